# Optimizing a Trainium2 kernel written in Bass

```python
import math
import jax, jax.numpy as jnp
from jax import lax
import numpy as np

D_MODEL = 1024
BATCH = 4
SEQ = 4096
DEPTH = 2

CTX_LEN = 256
GRID_W = 64
Q_BLOCK = 128
ROPE_THETA = 10000.0
EPS = 1e-6

MLA_HEADS = 4
MLA_Q_RANK = 384
MLA_KV_RANK = 256
MLA_NOPE = 64
MLA_ROPE = 32
MLA_V = 64
NA_HEADS = 4
NA_DIM = 64
NA_ROWS = 8
NA_COLS = 16
DIFF_HEADS = 4
DIFF_QK = 32
DIFF_V = 64
GQA_HEADS = 4
GQA_KV_HEADS = 2
GQA_DIM = 64
N_BRANCH = 4
BRANCH_W = 256
D_FF = -(-8 * D_MODEL // (3 * 256)) * 256
DEEPNORM_ALPHA = (2 * DEPTH) ** 0.25
DEEPNORM_BETA = (8 * DEPTH) ** -0.25

IN_SIZES = (
    MLA_Q_RANK, MLA_KV_RANK, MLA_ROPE,
    NA_HEADS * NA_DIM, NA_HEADS * NA_DIM, NA_HEADS * NA_DIM,
    DIFF_HEADS * 2 * DIFF_QK, DIFF_HEADS * 2 * DIFF_QK, DIFF_HEADS * DIFF_V,
    GQA_HEADS * GQA_DIM, GQA_KV_HEADS * GQA_DIM, GQA_KV_HEADS * GQA_DIM,
    N_BRANCH * D_MODEL,
)
D_IN = sum(IN_SIZES)

kernel_name = 'hybrid_parallel_mixer_dit_block'


def layer_norm(x, g, b):
    xf = x.astype(jnp.float32)
    mu = jnp.mean(xf, axis=-1, keepdims=True)
    var = jnp.mean(jnp.square(xf - mu), axis=-1, keepdims=True)
    return ((xf - mu) * lax.rsqrt(var + EPS) * g + b).astype(x.dtype)


def rms_norm(x, g):
    xf = x.astype(jnp.float32)
    y = xf * lax.rsqrt(jnp.mean(jnp.square(xf), axis=-1, keepdims=True) + EPS)
    return (y * g).astype(x.dtype)


def rope_tables(S, rot_dim, dtype):
    t = jnp.arange(S)
    pos = jnp.stack([t // GRID_W, t % GRID_W], axis=-1).astype(jnp.float32)
    n_f = rot_dim // 4
    inv = ROPE_THETA ** (-jnp.arange(n_f, dtype=jnp.float32) / n_f)
    ang = pos[:, :, None] * inv
    return jnp.cos(ang).astype(dtype), jnp.sin(ang).astype(dtype)


def apply_rope(x, tab):
    cos, sin = tab
    shp = x.shape
    n_f = shp[-1] // 4
    xr = x.reshape(shp[:-1] + (2, 2, n_f))
    x1, x2 = xr[..., 0, :], xr[..., 1, :]
    bshape = (1, shp[1]) + (1,) * (x.ndim - 3) + (2, n_f)
    c, s = cos.reshape(bshape), sin.reshape(bshape)
    return jnp.stack([x1 * c - x2 * s, x1 * s + x2 * c], axis=-2).reshape(shp)


def attn_probs(q, k, scale):
    B, Q, H, d = q.shape
    Hk = k.shape[2]
    qg = q.reshape(B, Q, Hk, H // Hk, d)
    s = jnp.einsum('bqkgd,btkd->bkgqt', qg, k).astype(jnp.float32) * scale
    return jax.nn.softmax(s, axis=-1)


def attn_apply(p, v):
    o = jnp.einsum('bkgqt,btkd->bqkgd', p.astype(v.dtype), v)
    B, Q, Hk, G, dv = o.shape
    return o.reshape(B, Q, Hk * G, dv)


def dense_attention(q, k, v, scale):
    return attn_apply(attn_probs(q, k, scale), v)


def diff_attention(q, k, v, lam, g_sub, lam_init):
    scale = DIFF_QK ** -0.5
    p1 = attn_probs(q[:, :, :, 0], k[:, :, :, 0], scale)
    p2 = attn_probs(q[:, :, :, 1], k[:, :, :, 1], scale)
    p = p1 - lam[:, None, None, None] * p2
    return rms_norm(attn_apply(p, v), g_sub) * (1.0 - lam_init)


def sweep_query_blocks(fn, q):
    B, S = q.shape[:2]
    nb = S // Q_BLOCK
    qb = jnp.moveaxis(q.reshape((B, nb, Q_BLOCK) + q.shape[2:]), 1, 0)
    o = lax.map(fn, qb)
    return jnp.moveaxis(o, 0, 1).reshape((B, S) + o.shape[3:])


def neighbourhood_attention(q, k, v, k_ctx, v_ctx, rpb):
    B, S, H, d = q.shape
    rows = S // GRID_W
    kh, kw = min(NA_ROWS, rows), NA_COLS
    scale = d ** -0.5
    col = np.arange(GRID_W)
    cs = np.clip(col - kw // 2, 0, GRID_W - kw)
    col_idx = cs[:, None] + np.arange(kw)[None, :]
    col_off = col_idx - col[:, None] + (NA_COLS - 1)
    kg = k.reshape(B, rows, GRID_W, H, d)
    vg = v.reshape(B, rows, GRID_W, H, d)
    qr = jnp.moveaxis(q.reshape(B, rows, GRID_W, H, d), 1, 0)

    def row_block(args):
        r, q_row = args
        rs = jnp.clip(r - kh // 2, 0, rows - kh)
        kb = lax.dynamic_slice_in_dim(kg, rs, kh, axis=1)[:, :, col_idx]
        vb = lax.dynamic_slice_in_dim(vg, rs, kh, axis=1)[:, :, col_idx]
        row_off = rs + jnp.arange(kh) - r + (NA_ROWS - 1)
        bias = jnp.transpose(rpb[:, row_off][:, :, col_off], (0, 2, 1, 3))
        s_loc = jnp.einsum('bwhd,biwjhd->bhwij', q_row, kb).astype(jnp.float32) * scale + bias[None].astype(jnp.float32)
        s_ctx = jnp.einsum('bwhd,bchd->bhwc', q_row, k_ctx).astype(jnp.float32) * scale
        s = jnp.concatenate([s_loc.reshape(B, H, GRID_W, kh * kw), s_ctx], axis=-1)
        p = jax.nn.softmax(s, axis=-1).astype(v.dtype)
        p_loc = p[..., :kh * kw].reshape(B, H, GRID_W, kh, kw)
        p_ctx = p[..., kh * kw:]
        return jnp.einsum('bhwij,biwjhd->bwhd', p_loc, vb) + jnp.einsum('bhwc,bchd->bwhd', p_ctx, v_ctx)

    o = lax.map(row_block, (jnp.arange(rows), qr))
    return jnp.moveaxis(o, 0, 1).reshape(B, S, H, d)


def mixer_inputs(h, w_in, g_q_a, w_q_up, g_kv_a, w_kv_up, g_qn, g_kn, tabs):
    B, T, _ = h.shape
    split_at = [int(i) for i in np.cumsum(IN_SIZES)[:-1]]
    (cq, ckv, kpe, q_na, k_na, v_na, q_df, k_df, v_df, q_gq, k_gq, v_gq, gates) = jnp.split(h @ w_in, split_at, axis=-1)
    qa = (rms_norm(cq, g_q_a) @ w_q_up).reshape(B, T, MLA_HEADS, MLA_NOPE + MLA_ROPE)
    kva = (rms_norm(ckv, g_kv_a) @ w_kv_up).reshape(B, T, MLA_HEADS, MLA_NOPE + MLA_V)
    q_nope, q_pe = qa[..., :MLA_NOPE], qa[..., MLA_NOPE:]
    k_nope, va = kva[..., :MLA_NOPE], kva[..., MLA_NOPE:]
    kpe = kpe.reshape(B, T, 1, MLA_ROPE)
    q_df = q_df.reshape(B, T, DIFF_HEADS, 2, DIFF_QK)
    k_df = k_df.reshape(B, T, DIFF_HEADS, 2, DIFF_QK)
    q_gq = rms_norm(q_gq.reshape(B, T, GQA_HEADS, GQA_DIM), g_qn)
    k_gq = rms_norm(k_gq.reshape(B, T, GQA_KV_HEADS, GQA_DIM), g_kn)
    if tabs is not None:
        q_pe = apply_rope(q_pe, tabs[MLA_ROPE])
        kpe = apply_rope(kpe, tabs[MLA_ROPE])
        q_df = apply_rope(q_df, tabs[DIFF_QK])
        k_df = apply_rope(k_df, tabs[DIFF_QK])
        q_gq = apply_rope(q_gq, tabs[GQA_DIM])
        k_gq = apply_rope(k_gq, tabs[GQA_DIM])
    qa = jnp.concatenate([q_nope, q_pe], axis=-1)
    ka = jnp.concatenate([k_nope, jnp.broadcast_to(kpe, (B, T, MLA_HEADS, MLA_ROPE))], axis=-1)
    return (qa, ka, va,
            q_na.reshape(B, T, NA_HEADS, NA_DIM), k_na.reshape(B, T, NA_HEADS, NA_DIM), v_na.reshape(B, T, NA_HEADS, NA_DIM),
            q_df, k_df, v_df.reshape(B, T, DIFF_HEADS, DIFF_V),
            q_gq, k_gq, v_gq.reshape(B, T, GQA_KV_HEADS, GQA_DIM),
            gates)


def merge_branches(gates, ys, w_branch, w_out):
    B, T, _ = gates.shape
    g = jax.nn.sigmoid(gates.reshape(B, T, N_BRANCH, D_MODEL))
    acc = g[:, :, 0] * (ys[0].reshape(B, T, BRANCH_W) @ w_branch[0])
    for i in range(1, N_BRANCH):
        acc = acc + g[:, :, i] * (ys[i].reshape(B, T, BRANCH_W) @ w_branch[i])
    return acc @ w_out


def token_mixers(hl, hc, tabs, w_in, g_q_a, w_q_up, g_kv_a, w_kv_up, rpb, lam_q1, lam_k1, lam_q2, lam_k2,
                 g_sub, g_qn, g_kn, w_branch, w_out, lam_init, need_ctx):
    proj = (w_in, g_q_a, w_q_up, g_kv_a, w_kv_up, g_qn, g_kn)
    (qa, ka, va, qb, kb, vb, qc, kc, vc, qd, kd, vd, gl) = mixer_inputs(hl, *proj, tabs)
    (xqa, xka, xva, xqb, xkb, xvb, xqc, xkc, xvc, xqd, xkd, xvd, gx) = mixer_inputs(hc, *proj, None)
    lam = (jnp.exp(jnp.sum(lam_q1 * lam_k1, axis=-1).astype(jnp.float32))
           - jnp.exp(jnp.sum(lam_q2 * lam_k2, axis=-1).astype(jnp.float32)) + lam_init)
    cat = lambda a, b: jnp.concatenate([a, b], axis=1)
    ka_all, va_all = cat(ka, xka), cat(va, xva)
    kc_all, vc_all = cat(kc, xkc), cat(vc, xvc)
    kd_all, vd_all = cat(kd, xkd), cat(vd, xvd)
    mla_scale = (MLA_NOPE + MLA_ROPE) ** -0.5
    ya = sweep_query_blocks(lambda q: dense_attention(q, ka_all, va_all, mla_scale), qa)
    yb = neighbourhood_attention(qb, kb, vb, xkb, xvb, rpb)
    yc = sweep_query_blocks(lambda q: diff_attention(q, kc_all, vc_all, lam, g_sub, lam_init), qc)
    yd = sweep_query_blocks(lambda q: dense_attention(q, kd_all, vd_all, GQA_DIM ** -0.5), qd)
    out_l = merge_branches(gl, (ya, yb, yc, yd), w_branch, w_out)
    if not need_ctx:
        return out_l, None
    ya_c = dense_attention(xqa, xka, xva, mla_scale)
    yb_c = dense_attention(xqb, xkb, xvb, NA_DIM ** -0.5)
    yc_c = diff_attention(xqc, xkc, xvc, lam, g_sub, lam_init)
    yd_c = dense_attention(xqd, xkd, xvd, GQA_DIM ** -0.5)
    out_c = merge_branches(gx, (ya_c, yb_c, yc_c, yd_c), w_branch, w_out)
    return out_l, out_c


def swiglu(h, w_gate_up, w_down):
    g, u = jnp.split(h @ w_gate_up, 2, axis=-1)
    return (jax.nn.silu(g) * u) @ w_down


def setup_inputs(seed: int = 0) -> dict:
    key = jax.random.key(seed)
    ks = jax.random.split(key, 27)
    f32 = jnp.float32
    nrm = lambda k, shape, s: jax.random.normal(k, shape, f32) * s
    gain = lambda k, shape: 1.0 + 0.01 * jax.random.normal(k, shape, f32)
    L = DEPTH
    return {
        'x': nrm(ks[0], (BATCH, SEQ, D_MODEL), 1.0),
        'c': nrm(ks[1], (BATCH, D_MODEL), 1.0),
        'ctx': nrm(ks[2], (BATCH, CTX_LEN, D_MODEL), 1.0),
        'c_ctx': nrm(ks[3], (D_MODEL,), 1.0),
        'w_ada': nrm(ks[4], (L, D_MODEL, 6 * D_MODEL), 0.5 * D_MODEL ** -0.5),
        'b_ada': nrm(ks[5], (L, 6 * D_MODEL), 0.01),
        'w_in': nrm(ks[6], (L, D_MODEL, D_IN), D_MODEL ** -0.5),
        'g_q_a': gain(ks[7], (L, MLA_Q_RANK)),
        'w_q_up': nrm(ks[8], (L, MLA_Q_RANK, MLA_HEADS * (MLA_NOPE + MLA_ROPE)), MLA_Q_RANK ** -0.5),
        'g_kv_a': gain(ks[9], (L, MLA_KV_RANK)),
        'w_kv_up': nrm(ks[10], (L, MLA_KV_RANK, MLA_HEADS * (MLA_NOPE + MLA_V)), MLA_KV_RANK ** -0.5),
        'rpb': nrm(ks[11], (L, NA_HEADS, 2 * NA_ROWS - 1, 2 * NA_COLS - 1), 0.1),
        'lam_q1': nrm(ks[12], (L, DIFF_HEADS, DIFF_QK), 0.1),
        'lam_k1': nrm(ks[13], (L, DIFF_HEADS, DIFF_QK), 0.1),
        'lam_q2': nrm(ks[14], (L, DIFF_HEADS, DIFF_QK), 0.1),
        'lam_k2': nrm(ks[15], (L, DIFF_HEADS, DIFF_QK), 0.1),
        'g_sub': gain(ks[16], (L, DIFF_V)),
        'g_qn': gain(ks[17], (L, GQA_DIM)),
        'g_kn': gain(ks[18], (L, GQA_DIM)),
        'w_branch': nrm(ks[19], (L, N_BRANCH, BRANCH_W, D_MODEL), BRANCH_W ** -0.5),
        'w_out': nrm(ks[20], (L, D_MODEL, D_MODEL), DEEPNORM_BETA * D_MODEL ** -0.5),
        'ln1_g': gain(ks[21], (L, D_MODEL)),
        'ln1_b': nrm(ks[22], (L, D_MODEL), 0.01),
        'w_gate_up': nrm(ks[23], (L, D_MODEL, 2 * D_FF), D_MODEL ** -0.5),
        'w_down': nrm(ks[24], (L, D_FF, D_MODEL), DEEPNORM_BETA * D_FF ** -0.5),
        'ln2_g': gain(ks[25], (L, D_MODEL)),
        'ln2_b': nrm(ks[26], (L, D_MODEL), 0.01),
    }


def reference(x, c, ctx, c_ctx, w_ada, b_ada, w_in, g_q_a, w_q_up, g_kv_a, w_kv_up, rpb,
              lam_q1, lam_k1, lam_q2, lam_k2, g_sub, g_qn, g_kn, w_branch, w_out,
              ln1_g, ln1_b, w_gate_up, w_down, ln2_g, ln2_b):
    S = x.shape[1]
    tabs = {d: rope_tables(S, d, x.dtype) for d in (MLA_ROPE, DIFF_QK, GQA_DIM)}
    s_lat = jax.nn.silu(c)
    s_ctx = jax.nn.silu(c_ctx)
    xl, xc = x, ctx
    for l in range(DEPTH):
        need_ctx = l < DEPTH - 1
        lam_init = 0.8 - 0.6 * math.exp(-0.3 * l)
        mod_l = (s_lat @ w_ada[l] + b_ada[l])[:, None, :]
        mod_c = (s_ctx @ w_ada[l] + b_ada[l])[None, None, :]
        sh1, sc1, g1, sh2, sc2, g2 = jnp.split(mod_l, 6, axis=-1)
        xsh1, xsc1, xg1, xsh2, xsc2, xg2 = jnp.split(mod_c, 6, axis=-1)
        hl = xl * (1.0 + sc1) + sh1
        hc = xc * (1.0 + xsc1) + xsh1
        out_l, out_c = token_mixers(hl, hc, tabs, w_in[l], g_q_a[l], w_q_up[l], g_kv_a[l], w_kv_up[l], rpb[l],
                                    lam_q1[l], lam_k1[l], lam_q2[l], lam_k2[l], g_sub[l], g_qn[l], g_kn[l],
                                    w_branch[l], w_out[l], lam_init, need_ctx)
        xl = layer_norm(DEEPNORM_ALPHA * xl + g1 * out_l, ln1_g[l], ln1_b[l])
        hl = xl * (1.0 + sc2) + sh2
        xl = layer_norm(DEEPNORM_ALPHA * xl + g2 * swiglu(hl, w_gate_up[l], w_down[l]), ln2_g[l], ln2_b[l])
        if need_ctx:
            xc = layer_norm(DEEPNORM_ALPHA * xc + xg1 * out_c, ln1_g[l], ln1_b[l])
            hc = xc * (1.0 + xsc2) + xsh2
            xc = layer_norm(DEEPNORM_ALPHA * xc + xg2 * swiglu(hc, w_gate_up[l], w_down[l]), ln2_g[l], ln2_b[l])
    return xl
```

```python
import math
import numpy as np
from contextlib import ExitStack
import concourse.bass as bass
import concourse.mybir as mybir
from concourse.bass_utils import run_bass_kernel_spmd

F32 = mybir.dt.float32
BF16 = mybir.dt.bfloat16
AF = mybir.ActivationFunctionType
ALU = mybir.AluOpType

SEM_LIMIT = 30000
DMA_SLOTS = 8

D = 1024
DEPTH = 2
TK = 4352
TQ = 2304
DFF = 2816
EPS = 1e-6
ALPHA = (2 * DEPTH) ** 0.25
NEG = -30000.0
CQ, CKV, KPE, QB, KB, VB, QC, KC, VC, QD, KD, VD = 0, 384, 640, 672, 928, 1184, 1440, 1696, 1952, 2208, 2464, 2592
NQKV = 2720


class Buf:
    __slots__ = ("name", "w", "r", "excl")

    def __init__(self, name, excl=False):
        self.name = name
        self.w = None
        self.r = {}
        self.excl = excl


class Prog:
    ENGS = ("pe", "act", "dve", "pool", "sp")

    def __init__(self, nc, es):
        self.nc = nc
        self.es = es
        self.sems = []
        self.q = {e: [] for e in self.ENGS}
        self.cur = {}
        self.cnt = {}
        for e in self.ENGS:
            self.cur[e] = self._new_sem("c_" + e)
            self.cnt[e] = 0
        self.seen = {e: {} for e in self.ENGS}
        self.dma_sem = {}
        self.dma_i = {}
        self.last = {}

    def _new_sem(self, name):
        h = self.es.enter_context(self.nc.semaphore(name + "_%d" % len(self.sems)))
        self.sems.append(h)
        return len(self.sems) - 1

    def _deps(self, reads, writes):
        waits = {}
        for b in reads:
            if b.w is not None and waits.get(b.w[0], 0) < b.w[1]:
                waits[b.w[0]] = b.w[1]
            if b.excl:
                for k, v in b.r.items():
                    if waits.get(k, 0) < v:
                        waits[k] = v
        for b in writes:
            if b.w is not None and waits.get(b.w[0], 0) < b.w[1]:
                waits[b.w[0]] = b.w[1]
            for k, v in b.r.items():
                if waits.get(k, 0) < v:
                    waits[k] = v
        return waits

    def _prune(self, eng, waits, skip_own=False):
        seen = self.seen[eng]
        wl = []
        for k, v in waits.items():
            if k == self.cur.get(eng):
                if skip_own or v < self.cnt[eng] - 1:
                    continue
            if seen.get(k, 0) >= v:
                continue
            seen[k] = v
            wl.append((k, v))
        return wl

    def _commit(self, ev, reads, writes):
        k, v = ev
        self.last[k] = v
        for b in writes:
            b.w = ev
            b.r = {}
        for b in reads:
            if b in writes:
                continue
            if b.excl:
                b.w = ev
                b.r = {}
            elif b.r.get(k, 0) < v:
                b.r[k] = v

    def op(self, eng, fn, reads=(), writes=()):
        waits = self._deps(reads, writes)
        wl = self._prune(eng, waits, skip_own=(eng == "pe"))
        if self.cnt[eng] >= SEM_LIMIT:
            self.cur[eng] = self._new_sem("c_" + eng)
            self.cnt[eng] = 0
        self.cnt[eng] += 1
        ev = (self.cur[eng], self.cnt[eng])
        self.q[eng].append((wl, fn, ev[0], 1))
        self._commit(ev, reads, writes)
        return ev

    def dma(self, queue, out, in_, reads=(), writes=(), fn=None):
        waits = self._deps(reads, writes)
        i = self.dma_i.get(queue, 0)
        self.dma_i[queue] = i + 1
        slot = i % DMA_SLOTS
        rnd = i // DMA_SLOTS
        per_gen = SEM_LIMIT // 16 - 2
        gen = rnd // per_gen
        kk = rnd % per_gen
        key = (queue, slot, gen)
        if key not in self.dma_sem:
            self.dma_sem[key] = self._new_sem("d_%s%d" % (queue, slot))
        sk = self.dma_sem[key]
        if kk > 0:
            if waits.get(sk, 0) < 16 * kk:
                waits[sk] = 16 * kk
        elif gen > 0:
            pk = self.dma_sem[(queue, slot, gen - 1)]
            if waits.get(pk, 0) < 16 * per_gen:
                waits[pk] = 16 * per_gen
        wl = self._prune(queue, waits)
        ev = (sk, 16 * (kk + 1))
        if fn is None:
            fn = (lambda e: e.dma_start(out=out, in_=in_))
        self.q[queue].append((wl, fn, sk, 16))
        self._commit(ev, reads, writes)
        return ev

    def barrier(self):
        evs = dict(self.last)
        for e in self.ENGS:
            wl = self._prune(e, dict(evs))
            if wl:
                self.q[e].append((wl, None, None, 0))

    def emit(self):
        nc = self.nc
        sems = self.sems
        q = self.q

        def run(e, lst):
            for wl, fn, sk, inc in lst:
                for k, v in wl:
                    e.wait_ge(sems[k], v)
                if fn is not None:
                    fn(e).then_inc(sems[sk], inc)

        with nc.Block() as block:
            @block.tensor
            def _(e):
                run(e, q["pe"])

            @block.scalar
            def _(e):
                run(e, q["act"])

            @block.vector
            def _(e):
                run(e, q["dve"])

            @block.gpsimd
            def _(e):
                run(e, q["pool"])

            @block.sync
            def _(e):
                run(e, q["sp"])


class Ring:
    def __init__(self, alloc, name, n, shape, dtype, excl=False):
        self.t = [alloc(name + str(i), shape, dtype) for i in range(n)]
        self.b = [Buf(name + str(i), excl) for i in range(n)]
        self.i = 0

    def next(self):
        j = self.i % len(self.t)
        self.i += 1
        return self.t[j], self.b[j]


class Scope:
    def __init__(self, P):
        self.P = P
        self.es = ExitStack()

    def __enter__(self):
        self.es.__enter__()
        return self

    def __exit__(self, *a):
        self.P.barrier()
        return self.es.__exit__(*a)

    _uid = [0]

    def sb(self, name, shape, dtype):
        Scope._uid[0] += 1
        return self.es.enter_context(self.P.nc.sbuf_tensor("%s_u%d" % (name, Scope._uid[0]), list(shape), dtype))

    def ps(self, name, shape, dtype):
        Scope._uid[0] += 1
        return self.es.enter_context(self.P.nc.psum_tensor("%s_u%d" % (name, Scope._uid[0]), list(shape), dtype))


def MM(P, out, lhsT, rhs, start, stop, r, w, **kw):
    return P.op("pe", lambda e: e.matmul(out, lhsT=lhsT, rhs=rhs, start=start, stop=stop, **kw), r, w)


def TR(P, out, in_, ident, r, w):
    return P.op("pe", lambda e: e.transpose(out=out, in_=in_, identity=ident), r, w)


def ACTV(P, out, in_, func, r, w, bias=None, scale=None):
    kw = {}
    if bias is not None:
        kw["bias"] = bias
    if scale is not None:
        kw["scale"] = scale
    return P.op("act", lambda e: e.activation(out=out, in_=in_, func=func, **kw), r, w)


def TT(P, eng, out, in0, in1, op, r, w):
    return P.op(eng, lambda e: e.tensor_tensor(out=out, in0=in0, in1=in1, op=op), r, w)


def TS(P, eng, out, in0, s1, s2, op0, op1, r, w):
    if s2 is None:
        return P.op(eng, lambda e: e.tensor_scalar(out=out, in0=in0, scalar1=s1, scalar2=None, op0=op0), r, w)
    return P.op(eng, lambda e: e.tensor_scalar(out=out, in0=in0, scalar1=s1, scalar2=s2, op0=op0, op1=op1), r, w)


def STT(P, eng, out, in0, scalar, in1, op0, op1, r, w):
    return P.op(eng, lambda e: e.scalar_tensor_tensor(out=out, in0=in0, scalar=scalar, in1=in1, op0=op0, op1=op1), r, w)


def CP(P, eng, out, in_, r, w):
    if eng == "act":
        return P.op("act", lambda e: e.copy(out=out, in_=in_), r, w)
    return P.op(eng, lambda e: e.tensor_copy(out=out, in_=in_), r, w)


def RECIP(P, out, in_, r, w):
    return P.op("dve", lambda e: e.reciprocal(out=out, in_=in_), r, w)


def MEMSET(P, eng, ap, val, r, w):
    return P.op(eng, lambda e: e.memset(ap, val), r, w)


class Builder:
    def __init__(self, layers):
        self.layers = list(layers)
        self.nc = bass.Bass("TRN2", target_bir_lowering=False)
        self.es = ExitStack()

    def din(self, name, shape, dtype=F32):
        return self.nc.dram_tensor(name, list(shape), dtype, kind="ExternalInput").ap()

    def dout(self, name, shape, dtype=F32):
        return self.nc.dram_tensor(name, list(shape), dtype, kind="ExternalOutput").ap()

    def dscr(self, name, shape, dtype=F32):
        return self.nc.dram_tensor(name, list(shape), dtype).ap()

    def build(self):
        nc = self.nc
        with self.es:
            P = self.P = Prog(nc, self.es)
            self.x_own = self.din("x_own", [2048, D])
            self.x_oth = self.din("x_oth", [2048, D])
            self.ctx_in = self.din("ctx_in", [256, D])
            self.cT = self.din("cT", [128, 16])
            self.consts_d = self.din("consts", [5, 128, 128])
            self.rope_d = self.din("rope", [4, 128, TK])
            self.W = {}
            for l in self.layers:
                w = {}
                s = "_%d" % l
                w["w_ada"] = self.din("w_ada" + s, [D, 6 * D])
                w["b_ada"] = self.din("b_ada" + s, [6 * D])
                w["w_qkv"] = self.din("w_qkv" + s, [D, NQKV])
                w["w_gate"] = self.din("w_gate" + s, [D, 4 * D])
                w["gqaT"] = self.din("gqaT" + s, [128, 3])
                w["w_q_up"] = self.din("w_q_up" + s, [384, 384])
                w["gkvT"] = self.din("gkvT" + s, [128, 2])
                w["w_kv_up"] = self.din("w_kv_up" + s, [256, 512])
                w["lamv"] = self.din("lamv" + s, [4, 128])
                w["gcols"] = self.din("gcols" + s, [128, 3])
                w["w_branch"] = self.din("w_branch" + s, [4, 256, D])
                w["w_out"] = self.din("w_out" + s, [D, D])
                w["ln1_g"] = self.din("ln1_g" + s, [D])
                w["ln1_b"] = self.din("ln1_b" + s, [D])
                w["w_gu"] = self.din("w_gu" + s, [D, 2 * DFF])
                w["w_down"] = self.din("w_down" + s, [DFF, D])
                w["ln2_g"] = self.din("ln2_g" + s, [D])
                w["ln2_b"] = self.din("ln2_b" + s, [D])
                w["nab"] = self.din("nab" + s, [3, 8, 4, 128, 512])
                self.W[l] = w
            self.y_out = self.dout("y_out", [2048, D])
            self.fused = (len(self.layers) == 2)
            if not self.fused:
                self.ctx_out = self.dout("ctx_out", [256, D])
            else:
                self.selmask_d = self.din("selmask", [128, D], mybir.dt.uint32)
                self.XO = self.dscr("XO", [2048, D])
                self.CO = self.dscr("CO", [256, D])
            self.modv = self.dscr("modv", [2, 6 * D])
            self.QA = self.dscr("QA", [4, 96, TQ], BF16)
            self.KA = self.dscr("KA", [4, 96, TK], BF16)
            self.VA = self.dscr("VA", [TK, 256], BF16)
            self.QBs = self.dscr("QBs", [256, TQ], BF16)
            self.KBs = self.dscr("KBs", [256, TK], BF16)
            self.VBs = self.dscr("VBs", [TK, 256], BF16)
            self.QCs = self.dscr("QCs", [256, TQ], BF16)
            self.KCs = self.dscr("KCs", [256, TK], BF16)
            self.VCs = self.dscr("VCs", [TK, 256], BF16)
            self.QDs = self.dscr("QDs", [256, TQ], BF16)
            self.KDs = self.dscr("KDs", [128, TK], BF16)
            self.VDs = self.dscr("VDs", [TK, 128], BF16)
            self.G = self.dscr("G", [32, 128, TQ])
            self.XL1 = self.dscr("XL1", [TQ, D])
            self.ACTT = self.dscr("ACTT", [22, 128, TQ], BF16)
            self.b_scr = {}
            self.b_yout = Buf('yout'); self.b_cout = Buf('cout')

            sbp = lambda n, s, d: self.es.enter_context(nc.sbuf_tensor(n, list(s), d))
            self.ident = sbp("ident", [128, 128], BF16)
            self.R32 = sbp("R32", [128, 128], BF16)
            self.R64 = sbp("R64", [128, 128], BF16)
            self.ones_f = sbp("ones_f", [128, 128], F32)
            self.blk_f = sbp("blk_f", [128, 128], F32)
            self.eps_col = sbp("eps_col", [128, 1], F32)
            self.b_const = Buf("const")
            P.dma("pool", self.ident[:], self.consts_d[0], writes=[self.b_const])
            P.dma("pool", self.R32[:], self.consts_d[1], writes=[self.b_const])
            P.dma("pool", self.R64[:], self.consts_d[2], writes=[self.b_const])
            P.dma("sp", self.ones_f[:], self.consts_d[3], writes=[self.b_const])
            P.dma("sp", self.blk_f[:], self.consts_d[4], writes=[self.b_const])
            MEMSET(P, "dve", self.eps_col[:], EPS, [], [self.b_const])
            self.wq16 = sbp("wq16", [128, 3, 416], BF16)
            self.wkv16 = sbp("wkv16", [128, 2, 512], BF16)
            self.lamt = sbp("lamt", [128, 8], F32)
            self.gcols = sbp("gcols", [128, 3], F32)
            self.b_small = Buf("small")
            MEMSET(P, "pool", self.wq16[:], 0.0, [], [self.b_small])

            ext = lambda ap: dict(ap=ap, deps=[])
            if not self.fused:
                l = self.layers[0]
                self.layer(l, l < DEPTH - 1, ext(self.x_own), ext(self.x_oth), ext(self.ctx_in),
                           (self.y_out, self.b_yout), (self.ctx_out, self.b_cout))
            else:
                self.selmask = sbp("selmask_sb", [128, D], mybir.dt.uint32)
                P.dma("sp", self.selmask[:], self.selmask_d, writes=[self.b_const])
                bXO, bCO, bXG = Buf("XO"), Buf("CO"), Buf("XG")
                self.layer(0, True, ext(self.x_own), ext(self.x_oth), ext(self.ctx_in), (self.XO, bXO), (self.CO, bCO))
                XO = self.XO
                XGs = [self.dscr("XG%d" % c_, [512, D]) for c_ in range(8)]
                for c_ in range(8):
                    P.op("pool", lambda e, c_=c_: e.collective_compute(
                        "AllGather", ALU.bypass, replica_groups=[[0, 1], [2, 3], [4, 5], [6, 7]],
                        ins=[XO[c_ * 256:(c_ + 1) * 256, :]], outs=[XGs[c_]]), [bXO], [bXG])

                def get_oth(r0):
                    c_, r = r0 // 256, r0 % 256
                    return XGs[c_][256 + r:256 + r + 128, :], XGs[c_][r:r + 128, :]
                self.layer(1, False, dict(ap=self.XO, deps=[bXO]),
                           dict(get=get_oth, deps=[bXG]),
                           dict(ap=self.CO, deps=[bCO]), (self.y_out, self.b_yout), None)
            P.barrier()
            P.emit()
        return nc

    def layer(self, l, need_ctx, x_own, x_oth, ctx_in, y_out, ctx_out):
        P = self.P
        W = self.W[l]
        with Scope(P) as GS:
            wq16_ = GS.sb("wqkv16", [128, 8, NQKV], BF16)
            bwq = Buf("wqkv")
            wv_ = W["w_qkv"].rearrange("(k p) n -> p k n", p=128)
            for k in range(8):
                P.dma("pool", wq16_[:, k, :], wv_[:, k, :], writes=[bwq])
            self.phase0(l)
            with Scope(P) as GS2:
                wg16 = GS2.sb("wg16", [128, 8, 4 * D], BF16)
                bwg = Buf("wg")
                self.phase1a(l, need_ctx, x_own, x_oth, ctx_in, (wg16, bwg, W["w_gate"]), wq16_, bwq)
                self.phase1b(l, need_ctx, x_own, ctx_in, wg16, bwg)
        with Scope(P) as YS:
            yT = YS.sb("yT", [128, 8, TQ], BF16)
            byT = Buf("yT")
            wb16 = YS.sb("wb16", [128, 8, D], BF16)
            wo16 = YS.sb("wo16", [128, 8, D], BF16)
            bwbo = Buf("wbo")
            P.dma("pool", wb16[:], W["w_branch"].rearrange("i (k p) n -> p (i k) n", p=128), writes=[bwbo])
            P.dma("pool", wo16[:], W["w_out"].rearrange("(k p) n -> p k n", p=128), writes=[bwbo])
            self.phase2(l, need_ctx, yT, byT)
            self.phase3(l, need_ctx, x_own, ctx_in, yT, byT, wb16, wo16, bwbo)
        with Scope(P) as DS:
            wd16 = DS.sb("wd16", [128, 22, D], BF16)
            bwd = Buf("wd")
            self.phase4a(l, need_ctx, (wd16, bwd, W["w_down"]))
            self.phase4b(l, need_ctx, y_out, ctx_out, wd16, bwd)

    def phase0(self, l):
        P, W = self.P, self.W[l]
        lam_init = 0.8 - 0.6 * math.exp(-0.3 * l)
        bmod = self.b_scr.setdefault("modv", Buf("modv"))
        with Scope(P) as S:
            sT = S.sb("sT", [128, 16], F32)
            bsT = Buf("sT")
            P.dma("sp", sT[:], self.cT, writes=[bsT])
            ACTV(P, sT[:], sT[:], AF.Silu, [bsT], [bsT])
            bb = S.sb("bb", [2, 6 * D], F32)
            bbb = Buf("bb")
            P.dma("sp", bb[:], W["b_ada"].partition_broadcast(2), writes=[bbb])
            mod = S.sb("mod", [2, 6 * D], F32)
            bm = Buf("mod")
            wr = Ring(S.sb, "wada", 2, [128, 8, 512], F32)
            pr = Ring(S.ps, "p0", 2, [128, 512], F32, excl=True)
            wv = W["w_ada"].rearrange("(k p) n -> p k n", p=128)
            sTv = sT[:].rearrange("p (k r) -> p k r", r=2)
            for n in range(12):
                wt, wb = wr.next()
                P.dma("sp", wt[:], wv[:, :, n * 512:(n + 1) * 512], writes=[wb])
                pt, pb = pr.next()
                for k in range(8):
                    MM(P, pt[0:2, :], sTv[:, k, :], wt[:, k, :], k == 0, k == 7, [bsT, wb], [pb])
                TT(P, "dve", mod[:, n * 512:(n + 1) * 512], pt[0:2, :], bb[:, n * 512:(n + 1) * 512], ALU.add,
                   [pb, bbb], [bm])
            TS(P, "dve", mod[:, 1024:2048], mod[:, 1024:2048], 1.0, None, ALU.add, None, [bm], [bm])
            TS(P, "dve", mod[:, 4096:5120], mod[:, 4096:5120], 1.0, None, ALU.add, None, [bm], [bm])
            P.dma("sp", self.modv, mod[:], reads=[bm], writes=[bmod])
            lv = S.sb("lv", [128, 4, 128], F32)
            blv = Buf("lv")
            for i in range(4):
                P.dma("sp", lv[:, i, :], W["lamv"][i].partition_broadcast(128), writes=[blv])
            pr1 = S.sb("pr1", [128, 2, 128], F32)
            bp1 = Buf("pr1")
            TT(P, "dve", pr1[:, 0, :], lv[:, 0, :], lv[:, 1, :], ALU.mult, [blv], [bp1])
            TT(P, "dve", pr1[:, 1, :], lv[:, 2, :], lv[:, 3, :], ALU.mult, [blv], [bp1])
            bs = self.b_small
            P.op("dve", lambda e: e.reduce_sum(out=self.lamt[:, 0:8], in_=pr1[:].rearrange("p a (h d) -> p (a h) d", d=32),
                                               axis=mybir.AxisListType.X), [bp1], [bs])
            ACTV(P, self.lamt[:, 0:8], self.lamt[:, 0:8], AF.Exp, [bs], [bs])
            STT(P, "dve", self.lamt[:, 0:4], self.lamt[:, 4:8], -lam_init, self.lamt[:, 0:4], ALU.add, ALU.subtract,
                [bs], [bs])
            P.dma("sp", self.gcols[:], W["gcols"], writes=[bs])
            TS(P, "dve", self.gcols[:, 0:1], self.gcols[:, 0:1], 1.0 - lam_init, None, ALU.mult, None, [bs], [bs])
            wq32 = S.sb("wq32", [128, 3, 384], F32)
            wkv32 = S.sb("wkv32", [128, 2, 512], F32)
            gq = S.sb("gq", [128, 3], F32)
            gkv = S.sb("gkv", [128, 2], F32)
            bw = Buf("wup")
            P.dma("sp", wq32[:], W["w_q_up"].rearrange("(k p) n -> p k n", p=128), writes=[bw])
            P.dma("sp", wkv32[:], W["w_kv_up"].rearrange("(k p) n -> p k n", p=128), writes=[bw])
            P.dma("sp", gq[:], W["gqaT"], writes=[bw])
            P.dma("sp", gkv[:], W["gkvT"], writes=[bw])
            for k in range(3):
                TS(P, "dve", self.wq16[:, k, 0:384], wq32[:, k, :], gq[:, k:k + 1], None, ALU.mult, None, [bw], [bs])
            for k in range(2):
                TS(P, "dve", self.wkv16[:, k, :], wkv32[:, k, :], gkv[:, k:k + 1], None, ALU.mult, None, [bw], [bs])

    def load_bc(self, t, src_row, b):
        self.P.dma("sp", t[:], src_row.partition_broadcast(128), reads=[self.b_scr["modv"]], writes=[b])

    def make_hT(self, S, xsrc, row0, N, A, Bc, bAB, xr, tr, hr, pTr, hTt, hTb):
        P = self.P
        for tt in range(N // 128):
            xt, xb = xr.next()
            r0 = row0 + tt * 128
            if "get" in xsrc:
                ap1, ap2 = xsrc["get"](r0)
            else:
                ap1, ap2 = xsrc["ap"][r0:r0 + 128, :], None
            P.dma("sp", xt[:], ap1, reads=xsrc["deps"], writes=[xb])
            if ap2 is not None:
                x2, x2b = xr.next()
                P.dma("sp", x2[:], ap2, reads=xsrc["deps"], writes=[x2b])
                P.op("dve", lambda e, xt=xt, x2=x2: e.copy_predicated(xt[:], self.selmask[:], x2[:]),
                     [x2b, self.b_const], [xb])
            t32, tb = tr.next()
            TT(P, "dve", t32[:], xt[:], A[:], ALU.mult, [xb, bAB], [tb])
            h16, hb = hr.next()
            TT(P, "dve", h16[:], t32[:], Bc[:], ALU.add, [tb, bAB], [hb])
            pT, pb = pTr.next()
            for k in range(8):
                TR(P, pT[:, k * 128:(k + 1) * 128], h16[:, k * 128:(k + 1) * 128], self.ident[:],
                   [hb, self.b_const], [pb])
            eng = "act" if tt % 2 == 0 else "dve"
            CP(P, eng, hTt[:, :, tt * 128:(tt + 1) * 128], pT[:].rearrange("p (k t) -> p k t", t=128), [pb], [hTb])

    def phase1a(self, l, need_ctx, x_own, x_oth, ctx_in, pre, w16, bw):
        P, W = self.P, self.W[l]
        bsm, bc = self.b_small, self.b_const
        bS = self.b_scr
        for nm in ("QA", "KA", "VA", "QB", "KB", "VB", "QC", "KC", "VC", "QD", "KD", "VD"):
            bS.setdefault(nm, Buf(nm))
        with Scope(P) as S:
            pw16, pbw, pwsrc = pre
            pwv = pwsrc.rearrange("(k p) n -> p k n", p=128)
            pre_k = [0]

            def prefetch_step():
                if pre_k[0] < 8:
                    k_ = pre_k[0]
                    pre_k[0] += 1
                    P.dma("pool", pw16[:, k_, :], pwv[:, k_, :], writes=[pbw])
            A_l = S.sb("A_l", [128, D], F32); B_l = S.sb("B_l", [128, D], F32)
            bAB = Buf("AB")
            self.load_bc(A_l, self.modv[0, 1024:2048], bAB)
            self.load_bc(B_l, self.modv[0, 0:1024], bAB)
            xr = Ring(S.sb, "xt", 3, [128, D], F32)
            tr = Ring(S.sb, "t32", 2, [128, D], F32)
            hr = Ring(S.sb, "h16", 2, [128, D], BF16)
            hTr = Ring(S.sb, "hT", 2, [128, 8, 512], BF16)
            pTr = Ring(S.ps, "pT", 2, [128, 1024], BF16, excl=True)
            pp = Ring(S.ps, "pp", 4, [128, 512], F32, excl=True)
            pm = Ring(S.ps, "pm", 2, [128, 512], F32, excl=True)
            rope_r = Ring(S.sb, "rope", 1, [128, 4, 512], F32)
            sq_r = Ring(S.sb, "sq", 1, [128, 3, 512], F32)
            cq_r = Ring(S.sb, "cqT", 1, [128, 3, 512], BF16)
            ckv_r = Ring(S.sb, "ckvT", 1, [128, 2, 512], BF16)
            rstd_r = Ring(S.sb, "rstd", 2, [128, 512], F32)
            o16 = Ring(S.sb, "o16", 8, [128, 512], BF16)
            f32r = Ring(S.sb, "f32t", 4, [128, 512], F32)
            rc_r = Ring(S.sb, "rcol", 4, [128, 1], F32)
            for t_, b_ in zip(o16.t, o16.b):
                MEMSET(P, "pool", t_[:], 0.0, [], [b_])

            chunks = []
            for ci in range(4):
                chunks.append((x_own, ci * 512, 512, ci * 512, ci * 512, False))
            for ci in range(4):
                chunks.append((x_oth, ci * 512, 512, 2048 + ci * 512, None, False))
            chunks.append((ctx_in, 0, 256, 4096, 2048 if need_ctx else None, True))
            ev_i = [0]

            def evac_eng():
                ev_i[0] += 1
                return "act" if ev_i[0] % 2 == 0 else "dve"

            for (xsrc, row0, N, koff, qoff, is_ctx) in chunks:
                prefetch_step()
                hTt, hTb = hTr.next()
                if is_ctx:
                    self.load_bc(A_l, self.modv[1, 1024:2048], bAB)
                    self.load_bc(B_l, self.modv[1, 0:1024], bAB)
                self.make_hT(S, xsrc, row0, N, A_l, B_l, bAB, xr, tr, hr, pTr, hTt, hTb)
                rp, rpb_ = rope_r.next()
                for i in range(4):
                    P.dma("sp", rp[:, i, 0:N], self.rope_d[i, :, koff:koff + N], writes=[rpb_])
                cos32, sin32, cos64, sin64 = rp[:, 0, :], rp[:, 1, :], rp[:, 2, :], rp[:, 3, :]

                def proj_fm(c0, M):
                    pt, pb = pp.next()
                    for k in range(8):
                        MM(P, pt[0:M, 0:N], w16[:, k, c0:c0 + M], hTt[:, k, 0:N], k == 0, k == 7, [bw, hTb], [pb])
                    return pt, pb

                def proj_tm(tt, c0, ncol):
                    pt, pb = pp.next()
                    for k in range(8):
                        MM(P, pt[:, 0:ncol], hTt[:, k, tt * 128:(tt + 1) * 128], w16[:, k, c0:c0 + ncol], k == 0, k == 7,
                           [bw, hTb], [pb])
                    return pt, pb

                def rstd_rep(sq_t, sq_b, nk, lhs, inv_n):
                    pt, pb = pm.next()
                    for k in range(nk):
                        MM(P, pt[:, 0:N], lhs, sq_t[:, k, 0:N], k == 0, k == nk - 1, [sq_b, bc], [pb])
                    rt, rb = rstd_r.next()
                    ACTV(P, rt[:, 0:N], pt[:, 0:N], AF.Sqrt, [pb, bc], [rb], bias=self.eps_col[:], scale=inv_n)
                    RECIP(P, rt[:, 0:N], rt[:, 0:N], [rb], [rb])
                    return rt, rb

                def rope(src16, sb_, p0, p1, R, cosT, sinT, dst16, db_):
                    pt, pb = pm.next()
                    MM(P, pt[:, 0:N], R[:, :], src16[:, 0:N], True, True, [sb_, bc], [pb])
                    t1, b1 = f32r.next()
                    TT(P, "pool", t1[p0:p1, 0:N], src16[p0:p1, 0:N], cosT[p0:p1, 0:N], ALU.mult, [sb_, rpb_], [b1])
                    t2, b2 = f32r.next()
                    TT(P, "dve", t2[p0:p1, 0:N], pt[p0:p1, 0:N], sinT[p0:p1, 0:N], ALU.mult, [pb, rpb_], [b2])
                    TT(P, "pool", dst16[p0:p1, 0:N], t1[p0:p1, 0:N], t2[p0:p1, 0:N], ALU.add, [b1, b2], [db_])

                own = qoff is not None
                if own:
                    sqt, sqb = sq_r.next()
                    cqt, cqb = cq_r.next()
                    for k in range(3):
                        pt, pb = proj_fm(CQ + k * 128, 128)
                        CP(P, "dve", cqt[:, k, 0:N], pt[:, 0:N], [pb], [cqb])
                        ACTV(P, sqt[:, k, 0:N], pt[:, 0:N], AF.Square, [pb], [sqb])
                    rq, rqb = rstd_rep(sqt, sqb, 3, self.ones_f[:], 1.0 / 384)
                    for h in range(4):
                        pt, pb = pp.next()
                        for k in range(3):
                            MM(P, pt[:, 0:N], self.wq16[:, k, h * 96:h * 96 + 128], cqt[:, k, 0:N], k == 0, k == 2,
                               [bsm, cqb], [pb])
                        qh, qb_ = o16.next()
                        TT(P, "dve", qh[0:96, 0:N], pt[0:96, 0:N], rq[0:96, 0:N], ALU.mult, [pb, rqb], [qb_])
                        rope(qh, qb_, 64, 96, self.R32, cos32, sin32, qh, qb_)
                        P.dma("pool", self.QA[h, :, qoff:qoff + N], qh[0:96, 0:N], reads=[qb_], writes=[bS["QA"]])
                sqt, sqb = sq_r.next()
                ckt, ckb = ckv_r.next()
                for k in range(2):
                    pt, pb = proj_fm(CKV + k * 128, 128)
                    CP(P, "dve", ckt[:, k, 0:N], pt[:, 0:N], [pb], [ckb])
                    ACTV(P, sqt[:, k, 0:N], pt[:, 0:N], AF.Square, [pb], [sqb])
                rkv, rkvb = rstd_rep(sqt, sqb, 2, self.ones_f[:], 1.0 / 256)
                pt, pb = proj_fm(KPE - 64, 128)
                kp, kpb = o16.next()
                CP(P, "act", kp[64:96, 0:N], pt[64:96, 0:N], [pb], [kpb])
                kr, krb = o16.next()
                rope(kp, kpb, 64, 96, self.R32, cos32, sin32, kr, krb)
                for h in range(4):
                    P.dma("pool", self.KA[h, 64:96, koff:koff + N], kr[64:96, 0:N], reads=[krb], writes=[bS["KA"]])
                for h in range(4):
                    pt, pb = pp.next()
                    for k in range(2):
                        MM(P, pt[:, 0:N], self.wkv16[:, k, h * 64:h * 64 + 128], ckt[:, k, 0:N], k == 0, k == 1,
                           [bsm, ckb], [pb])
                    kn, knb = o16.next()
                    TT(P, "dve", kn[0:64, 0:N], pt[0:64, 0:N], rkv[0:64, 0:N], ALU.mult, [pb, rkvb], [knb])
                    P.dma("pool", self.KA[h, 0:64, koff:koff + N], kn[0:64, 0:N], reads=[knb], writes=[bS["KA"]])
                for tt in range(N // 128):
                    pt, pb = pp.next()
                    for k in range(2):
                        MM(P, pt[:, 0:256], ckt[:, k, tt * 128:(tt + 1) * 128], self.wkv16[:, k, 256:512], k == 0, k == 1,
                           [bsm, ckb], [pb])
                    pc, pcb = pm.next()
                    for k in range(2):
                        MM(P, pc[:, 0:1], sqt[:, k, tt * 128:(tt + 1) * 128], self.ones_f[:, 0:1], k == 0, k == 1,
                           [sqb, bc], [pcb])
                    rc, rcb = rc_r.next()
                    ACTV(P, rc[:], pc[:, 0:1], AF.Sqrt, [pcb, bc], [rcb], bias=self.eps_col[:], scale=1.0 / 256)
                    RECIP(P, rc[:], rc[:], [rcb], [rcb])
                    vt, vb = o16.next()
                    TS(P, "dve", vt[:, 0:256], pt[:, 0:256], rc[:, 0:1], None, ALU.mult, None, [pb, rcb], [vb])
                    P.dma("pool", self.VA[koff + tt * 128: koff + (tt + 1) * 128, :], vt[:, 0:256], reads=[vb],
                          writes=[bS["VA"]])
                for c in range(2):
                    if own:
                        pt, pb = proj_fm(QB + c * 128, 128)
                        ot, ob = o16.next()
                        eg = evac_eng()
                        CP(P, eg, ot[:, 0:N], pt[:, 0:N], [pb], [ob])
                        P.dma("act" if eg == "act" else "pool", self.QBs[c * 128:(c + 1) * 128, qoff:qoff + N], ot[:, 0:N], reads=[ob],
                              writes=[bS["QB"]])
                    pt, pb = proj_fm(KB + c * 128, 128)
                    ot, ob = o16.next()
                    eg = evac_eng()
                    CP(P, eg, ot[:, 0:N], pt[:, 0:N], [pb], [ob])
                    P.dma("act" if eg == "act" else "pool", self.KBs[c * 128:(c + 1) * 128, koff:koff + N], ot[:, 0:N], reads=[ob], writes=[bS["KB"]])
                later = []

                def diff_stage_b(s16, sb_, dst, c, off, nm):
                    def run():
                        d16, db_ = o16.next()
                        rope(s16, sb_, 0, 128, self.R32, cos32, sin32, d16, db_)
                        P.dma("pool", dst[c * 128:(c + 1) * 128, off:off + N], d16[:, 0:N], reads=[db_], writes=[bS[nm]])
                    return run

                for c in range(2):
                    for (col, dst, nm, doit, off) in ((QC, self.QCs, "QC", own, qoff), (KC, self.KCs, "KC", True, koff)):
                        if not doit:
                            continue
                        pt, pb = proj_fm(col + c * 128, 128)
                        s16, sb_ = o16.next()
                        CP(P, evac_eng(), s16[:, 0:N], pt[:, 0:N], [pb], [sb_])
                        later.append(diff_stage_b(s16, sb_, dst, c, off, nm))
                        if len(later) > 1:
                            later.pop(0)()
                specs = []
                if own:
                    specs += [(QD, self.QDs, "QD", 0, 1, qoff), (QD + 128, self.QDs, "QD", 1, 1, qoff)]
                specs.append((KD, self.KDs, "KD", 0, 2, koff))
                for (col, dst, nm, c, gi, off) in specs:
                    pt, pb = proj_fm(col, 128)
                    sq1, sq1b = sq_r.next()
                    ACTV(P, sq1[:, 0, 0:N], pt[:, 0:N], AF.Square, [pb], [sq1b])
                    rr, rrb = rstd_rep(sq1, sq1b, 1, self.blk_f[:], 1.0 / 64)
                    s16, sb_ = o16.next()
                    STT(P, "dve", s16[:, 0:N], pt[:, 0:N], self.gcols[:, gi:gi + 1], rr[:, 0:N], ALU.mult, ALU.mult,
                        [pb, rrb, bsm], [sb_])
                    d16, db_ = o16.next()
                    rope(s16, sb_, 0, 128, self.R64, cos64, sin64, d16, db_)
                    P.dma("pool", dst[c * 128:(c + 1) * 128, off:off + N], d16[:, 0:N], reads=[db_], writes=[bS[nm]])
                while later:
                    later.pop(0)()
                for tt in range(N // 128):
                    for (col, ncol, dst, nm) in ((VB, 256, self.VBs, "VB"), (VC, 256, self.VCs, "VC"),
                                                 (VD, 128, self.VDs, "VD")):
                        pt, pb = proj_tm(tt, col, ncol)
                        vt, vb = o16.next()
                        eg = evac_eng()
                        CP(P, eg, vt[:, 0:ncol], pt[:, 0:ncol], [pb], [vb])
                        P.dma("act" if eg == "act" else "pool", dst[koff + tt * 128: koff + (tt + 1) * 128, :], vt[:, 0:ncol], reads=[vb],
                              writes=[bS[nm]])

    def phase1b(self, l, need_ctx, x_own, ctx_in, w16, bw):
        P, W = self.P, self.W[l]
        bG = self.b_scr.setdefault("G", Buf("G"))
        with Scope(P) as S:
            A_l = S.sb("A_l", [128, D], F32); B_l = S.sb("B_l", [128, D], F32)
            bAB = Buf("AB")
            self.load_bc(A_l, self.modv[0, 1024:2048], bAB)
            self.load_bc(B_l, self.modv[0, 0:1024], bAB)
            xr = Ring(S.sb, "xt", 3, [128, D], F32)
            tr = Ring(S.sb, "t32", 2, [128, D], F32)
            hr = Ring(S.sb, "h16", 2, [128, D], BF16)
            hTr = Ring(S.sb, "hT", 2, [128, 8, 512], BF16)
            pTr = Ring(S.ps, "pT", 2, [128, 1024], BF16, excl=True)
            pp = Ring(S.ps, "pp", 4, [128, 512], F32, excl=True)
            gr = Ring(S.sb, "gt", 4, [128, 512], F32)
            chunks = [(x_own, ci * 512, 512, ci * 512, False) for ci in range(4)]
            if need_ctx:
                chunks.append((ctx_in, 0, 256, 2048, True))
            for (xsrc, row0, N, qoff, is_ctx) in chunks:
                hTt, hTb = hTr.next()
                if is_ctx:
                    self.load_bc(A_l, self.modv[1, 1024:2048], bAB)
                    self.load_bc(B_l, self.modv[1, 0:1024], bAB)
                self.make_hT(S, xsrc, row0, N, A_l, B_l, bAB, xr, tr, hr, pTr, hTt, hTb)
                for ct in range(32):
                    pt, pb = pp.next()
                    for k in range(8):
                        MM(P, pt[:, 0:N], w16[:, k, ct * 128:(ct + 1) * 128], hTt[:, k, 0:N], k == 0, k == 7, [bw, hTb],
                           [pb])
                    gt, gb = gr.next()
                    ACTV(P, gt[:, 0:N], pt[:, 0:N], AF.Sigmoid, [pb], [gb])
                    P.dma("act", self.G[ct, :, qoff:qoff + N], gt[:, 0:N], reads=[gb], writes=[bG])

    def phase2(self, l, need_ctx, yT, byT):
        P, W = self.P, self.W[l]
        bS, bc, bsm = self.b_scr, self.b_const, self.b_small
        with Scope(P) as S:
            ktr = Ring(S.sb, "kt", 4, [128, TK], BF16)
            qtr = Ring(S.sb, "qt", 2, [128, TQ], BF16)
            for t_, b_ in zip(qtr.t, qtr.b):
                MEMSET(P, "pool", t_[:], 0.0, [], [b_])
            vaug = [S.sb("vaug0", [128, 34, 128], BF16), S.sb("vaug1", [128, 34, 128], BF16)]
            bva = [Buf("vaug0"), Buf("vaug1")]
            MEMSET(P, "pool", vaug[0][:, :, 64:128], 1.0, [], [bva[0]])
            MEMSET(P, "pool", vaug[1][:, :, 0:64], 1.0, [], [bva[1]])
            pS = Ring(S.ps, "pS", 2, [128, 1024], F32, excl=True)
            pO = Ring(S.ps, "pO", 2, [128, 1024], F32, excl=True)
            ptr = Ring(S.sb, "pT16", 3, [128, 1024], BF16)
            bir = Ring(S.sb, "bias", 4, [128, 512], F32)
            t32r = Ring(S.sb, "s32", 3, [128, 512], F32)
            recr = Ring(S.sb, "rec", 3, [128, 1024], F32)
            zr = Ring(S.sb, "z", 4, [128, 1024], F32)
            z2r = Ring(S.sb, "z2", 2, [128, 1024], F32)
            rsr = Ring(S.sb, "rs", 2, [128, 512], F32)
            for t_, b_ in zip(z2r.t, z2r.b):
                MEMSET(P, "pool", t_[:], 0.0, [], [b_])

            def attend(qt, qb, kt, kb, hp, q0, ncol, tiles, scale):
                va, vab = vaug[hp // 64], bva[hp // 64]
                po, pob = pO.next()
                nb = (ncol + 511) // 512
                nt = len(tiles)

                def pv(p16, p16b, kti, i):
                    for j in range(nb):
                        w_ = min(512, ncol - j * 512)
                        MM(P, po[:, j * 512: j * 512 + w_], va[:, kti, :], p16[:, j * 512: j * 512 + w_],
                           i == 0, i == nt - 1, [vab, p16b], [pob])

                pend = None
                for i, (kti, bias) in enumerate(tiles):
                    ps, psb = pS.next()
                    for j in range(nb):
                        w_ = min(512, ncol - j * 512)
                        MM(P, ps[:, j * 512: j * 512 + w_], kt[:, kti * 128:(kti + 1) * 128],
                           qt[:, q0 + j * 512: q0 + j * 512 + w_], True, True, [kb, qb], [psb])
                    if pend is not None:
                        pv(*pend)
                    p16, p16b = ptr.next()
                    if bias is None:
                        ACTV(P, p16[:, 0:ncol], ps[:, 0:ncol], AF.Exp, [psb], [p16b], scale=scale)
                    else:
                        bt, bb_ = bir.next()
                        P.dma("sp", bt[:], bias, writes=[bb_])
                        s32, s32b = t32r.next()
                        STT(P, "dve", s32[:, 0:ncol], ps[:, 0:ncol], scale, bt[:, 0:ncol], ALU.mult, ALU.add,
                            [psb, bb_], [s32b])
                        ACTV(P, p16[:, 0:ncol], s32[:, 0:ncol], AF.Exp, [s32b], [p16b])
                    pend = (p16, p16b, kti, i)
                pv(*pend)
                return po, pob

            def normalise(po, pob, hp, ncol, dst, dstb):
                sp_ = 64 - hp
                rt, rb = recr.next()
                RECIP(P, rt[sp_:sp_ + 64, 0:ncol], po[sp_:sp_ + 64, 0:ncol], [pob], [rb])
                TT(P, "dve", dst, po[hp:hp + 64, 0:ncol], rt[sp_:sp_ + 64, 0:ncol], ALU.mult, [pob, rb], [dstb])

            def load_v(Vsrc, vcol, hp, vname):
                va, vab = vaug[hp // 64], bva[hp // 64]
                vo = 0 if hp == 0 else 64
                P.dma("sp", va[:, :, vo:vo + 64], Vsrc.rearrange("(kt p) c -> p kt c", p=128)[:, :, vcol:vcol + 64],
                      reads=[bS[vname]], writes=[vab])

            def load_k(Ksrc, base, rows, kname):
                kt, kb = ktr.next()
                MEMSET(P, "pool", kt[:], 0.0, [], [kb])
                P.dma("sp", kt[base:base + rows, :], Ksrc, reads=[bS[kname]], writes=[kb])
                return kt, kb

            def load_q(Qsrc, rows, qname):
                qt, qb = qtr.next()
                P.dma("sp", qt[0:rows, :], Qsrc, reads=[bS[qname]], writes=[qb])
                return qt, qb

            all_k = [(i, None) for i in range(34)]
            ctx_k = [(32, None), (33, None)]
            dense_groups = [(0, 1024, all_k), (1024, 1024, all_k)]
            if need_ctx:
                dense_groups.append((2048, 256, ctx_k))

            sc_a = 96 ** -0.5
            for h in range(4):
                hp = (h % 2) * 64
                kt, kb = load_k(self.KA[h], 0, 96, "KA")
                qt, qb = load_q(self.QA[h], 96, "QA")
                load_v(self.VA, h * 64, hp, "VA")
                for (q0, ncol, tiles) in dense_groups:
                    po, pob = attend(qt, qb, kt, kb, hp, q0, ncol, tiles, sc_a)
                    normalise(po, pob, hp, ncol, yT[hp:hp + 64, 0 + h // 2, q0:q0 + ncol], byT)
            sc_b = 64 ** -0.5
            for c in range(2):
                qt, qb = load_q(self.QBs[c * 128:(c + 1) * 128, :], 128, "QB")
                for e_ in range(2):
                    h = 2 * c + e_
                    hp = e_ * 64
                    kt, kb = load_k(self.KBs[h * 64:(h + 1) * 64, :], hp, 64, "KB")
                    load_v(self.VBs, h * 64, hp, "VB")
                    groups = []
                    for j in range(4):
                        st = (0, 1, 1, 2)[j]
                        tiles = []
                        for s in range(8):
                            lt = 4 * j - 2 + s
                            if lt < 0:
                                lt += 32
                            tiles.append((lt, W["nab"][st, s, h]))
                        tiles += ctx_k
                        groups.append((j * 512, 512, tiles))
                    if need_ctx:
                        groups.append((2048, 256, ctx_k))
                    for (q0, ncol, tiles) in groups:
                        po, pob = attend(qt, qb, kt, kb, hp, q0, ncol, tiles, sc_b)
                        normalise(po, pob, hp, ncol, yT[hp:hp + 64, 2 + c, q0:q0 + ncol], byT)
            sc_c = 32 ** -0.5

            def diff_epilogue(h, hp, q0, ncol, z0, z0b, z1, z1b):
                def run():
                    STT(P, "dve", z0[hp:hp + 64, 0:ncol], z1[hp:hp + 64, 0:ncol], self.lamt[hp:hp + 64, h:h + 1],
                        z0[hp:hp + 64, 0:ncol], ALU.mult, ALU.add, [z1b, z0b, bsm], [z0b])
                    z2, z2b = z2r.next()
                    TT(P, "pool", z2[hp:hp + 64, 0:ncol], z0[hp:hp + 64, 0:ncol], z0[hp:hp + 64, 0:ncol], ALU.mult,
                       [z0b], [z2b])
                    for j in range((ncol + 511) // 512):
                        w_ = min(512, ncol - j * 512)
                        pm_, pmb = pS.next()
                        MM(P, pm_[:, 0:w_], self.blk_f[:], z2[:, j * 512:j * 512 + w_], True, True, [z2b, bc], [pmb])
                        rs, rsb = rsr.next()
                        ACTV(P, rs[hp:hp + 64, 0:w_], pm_[hp:hp + 64, 0:w_], AF.Sqrt, [pmb, bc], [rsb],
                             bias=self.eps_col[hp:hp + 64, :], scale=1.0 / 64)
                        RECIP(P, rs[hp:hp + 64, 0:w_], rs[hp:hp + 64, 0:w_], [rsb], [rsb])
                        STT(P, "dve", yT[hp:hp + 64, 4 + h // 2, q0 + j * 512: q0 + j * 512 + w_],
                            z0[hp:hp + 64, j * 512:j * 512 + w_], self.gcols[hp:hp + 64, 0:1], rs[hp:hp + 64, 0:w_],
                            ALU.mult, ALU.mult, [z0b, rsb, bsm], [byT])
                return run

            pending = None
            for h in range(4):
                hp = (h % 2) * 64
                if h % 2 == 0:
                    qt, qb = load_q(self.QCs[(h // 2) * 128:(h // 2 + 1) * 128, :], 128, "QC")
                kms = [load_k(self.KCs[h * 64 + m * 32: h * 64 + (m + 1) * 32, :], hp + m * 32, 32, "KC")
                       for m in range(2)]
                load_v(self.VCs, h * 64, hp, "VC")
                for (q0, ncol, tiles) in dense_groups:
                    zt = []
                    for m in range(2):
                        po, pob = attend(qt, qb, kms[m][0], kms[m][1], hp, q0, ncol, tiles, sc_c)
                        z, zb = zr.next()
                        normalise(po, pob, hp, ncol, z[hp:hp + 64, 0:ncol], zb)
                        zt.append((z, zb))
                        if m == 0 and pending is not None:
                            pending()
                            pending = None
                    pending = diff_epilogue(h, hp, q0, ncol, zt[0][0], zt[0][1], zt[1][0], zt[1][1])
            if pending is not None:
                pending()
            sc_d = 64 ** -0.5
            kds = [load_k(self.KDs[e_ * 64:(e_ + 1) * 64, :], e_ * 64, 64, "KD") for e_ in range(2)]
            for c in range(2):
                qt, qb = load_q(self.QDs[c * 128:(c + 1) * 128, :], 128, "QD")
                for e_ in range(2):
                    hp = e_ * 64
                    load_v(self.VDs, e_ * 64, hp, "VD")
                    for (q0, ncol, tiles) in dense_groups:
                        po, pob = attend(qt, qb, kds[e_][0], kds[e_][1], hp, q0, ncol, tiles, sc_d)
                        normalise(po, pob, hp, ncol, yT[hp:hp + 64, 6 + c, q0:q0 + ncol], byT)

    def resid_ln(self, S, ps_halves, xsrc_ap, gbc, lng, lnb, bbc, out_ap, out_buf, rings):
        P = self.P
        xr, tr, zr, str_, mvr, orr = rings
        xt, xb = xr.next()
        P.dma("sp", xt[:], xsrc_ap[0], reads=xsrc_ap[1], writes=[xb])
        t32, tb = tr.next()
        for nh, (pt, pb) in enumerate(ps_halves):
            TT(P, "dve", t32[:, nh * 512:(nh + 1) * 512], pt[:, 0:512], gbc[:, nh * 512:(nh + 1) * 512], ALU.mult,
               [pb, bbc], [tb])
        z, zb = zr.next()
        STT(P, "dve", z[:], xt[:], ALPHA, t32[:], ALU.mult, ALU.add, [xb, tb], [zb])
        st, stb = str_.next()
        P.op("dve", lambda e: e.bn_stats(out=st[:, 0:6], in_=z[:, 0:512]), [zb], [stb])
        P.op("dve", lambda e: e.bn_stats(out=st[:, 6:12], in_=z[:, 512:1024]), [zb], [stb])
        mv, mvb = mvr.next()
        P.op("dve", lambda e: e.bn_aggr(out=mv[:, 0:2], in_=st[:]), [stb], [mvb])
        ACTV(P, mv[:, 2:3], mv[:, 1:2], AF.Sqrt, [mvb, self.b_const], [mvb], bias=self.eps_col[:], scale=1.0)
        RECIP(P, mv[:, 2:3], mv[:, 2:3], [mvb], [mvb])
        TS(P, "dve", z[:], z[:], mv[:, 0:1], mv[:, 2:3], ALU.subtract, ALU.mult, [zb, mvb], [zb])
        TT(P, "pool", z[:], z[:], lng[:], ALU.mult, [zb, bbc], [zb])
        ot, ob = orr.next()
        TT(P, "pool", ot[:], z[:], lnb[:], ALU.add, [zb, bbc], [ob])
        P.dma("pool", out_ap, ot[:], reads=[ob], writes=[out_buf])

    def ln_rings(self, S):
        return (Ring(S.sb, "lx", 2, [128, D], F32), Ring(S.sb, "lt", 2, [128, D], F32),
                Ring(S.sb, "lz", 2, [128, D], F32), Ring(S.sb, "lst", 2, [128, 12], F32),
                Ring(S.sb, "lmv", 2, [128, 4], F32), Ring(S.sb, "lo", 2, [128, D], F32))

    def phase3(self, l, need_ctx, x_own, ctx_in, yT, byT, wb16, wo16, bw):
        P, W = self.P, self.W[l]
        bS = self.b_scr
        bX = bS.setdefault("XL1", Buf("XL1"))
        with Scope(P) as S:
            g_l = S.sb("g_l", [128, D], F32); g_c = S.sb("g_c", [128, D], F32)
            lng = S.sb("lng", [128, D], F32); lnb = S.sb("lnb", [128, D], F32)
            bbc = Buf("bc3")
            self.load_bc(g_l, self.modv[0, 2048:3072], bbc)
            self.load_bc(g_c, self.modv[1, 2048:3072], bbc)
            P.dma("sp", lng[:], W["ln1_g"].partition_broadcast(128), writes=[bbc])
            P.dma("sp", lnb[:], W["ln1_b"].partition_broadcast(128), writes=[bbc])
            pp = Ring(S.ps, "pp", 3, [128, 512], F32, excl=True)
            po = Ring(S.ps, "po", 4, [128, 512], F32, excl=True)
            gr = Ring(S.sb, "gt", 4, [128, 512], F32)
            accr = Ring(S.sb, "acc", 2, [128, 512], F32)
            tmr = Ring(S.sb, "tm", 2, [128, 512], F32)
            aTr = Ring(S.sb, "accT", 2, [128, 8, 512], BF16)
            rings = self.ln_rings(S)
            chunks = [(x_own, ci * 512, 512, ci * 512, g_l) for ci in range(4)]
            if need_ctx:
                chunks.append((ctx_in, 0, 256, 2048, g_c))
            for (xsrc, row0, N, qoff, gbc) in chunks:
                aT, aTb = aTr.next()
                for c in range(8):
                    acc, accb = accr.next()
                    for i in range(4):
                        pt, pb = pp.next()
                        for k in range(2):
                            MM(P, pt[:, 0:N], wb16[:, i * 2 + k, c * 128:(c + 1) * 128], yT[:, i * 2 + k, qoff:qoff + N],
                               k == 0, k == 1, [bw, byT], [pb])
                        gt, gb = gr.next()
                        P.dma("sp", gt[:, 0:N], self.G[i * 8 + c, :, qoff:qoff + N], reads=[bS["G"]], writes=[gb])
                        if i == 0:
                            TT(P, "dve", acc[:, 0:N], pt[:, 0:N], gt[:, 0:N], ALU.mult, [pb, gb], [accb])
                        else:
                            tm, tmb = tmr.next()
                            TT(P, "dve", tm[:, 0:N], pt[:, 0:N], gt[:, 0:N], ALU.mult, [pb, gb], [tmb])
                            if i < 3:
                                TT(P, "dve" if i == 1 else "pool", acc[:, 0:N], acc[:, 0:N], tm[:, 0:N], ALU.add,
                                   [accb, tmb], [accb])
                            else:
                                TT(P, "pool", aT[:, c, 0:N], acc[:, 0:N], tm[:, 0:N], ALU.add, [accb, tmb], [aTb])
                for tt in range(N // 128):
                    halves = []
                    for nh in range(2):
                        pt, pb = po.next()
                        for c in range(8):
                            MM(P, pt[:, 0:512], aT[:, c, tt * 128:(tt + 1) * 128], wo16[:, c, nh * 512:(nh + 1) * 512],
                               c == 0, c == 7, [aTb, bw], [pb])
                        halves.append((pt, pb))
                    r0 = row0 + tt * 128
                    self.resid_ln(S, halves, (xsrc["ap"][r0:r0 + 128, :], xsrc["deps"]), gbc, lng, lnb, bbc,
                                  self.XL1[qoff + tt * 128: qoff + (tt + 1) * 128, :], bX, rings)

    def phase4a(self, l, need_ctx, pre):
        P, W = self.P, self.W[l]
        bS = self.b_scr
        bA = bS.setdefault("ACTT", Buf("ACTT"))
        with Scope(P) as S:
            w16 = S.sb("wgu16", [128, 8, 2 * DFF], BF16)
            wv = W["w_gu"].rearrange("(k p) n -> p k n", p=128)
            JB = 4
            bwj = {}
            wblocks = []
            for j0 in range(0, 22, JB):
                j1 = min(22, j0 + JB)
                bb_ = Buf("wgu%d" % j0)
                wblocks.append((j0, j1, bb_))
                for j in range(j0, j1):
                    bwj[j] = bb_

            def issue_block(bi):
                j0, j1, bb_ = wblocks[bi]
                P.dma("pool", w16[:, :, j0 * 128:j1 * 128], wv[:, :, j0 * 128:j1 * 128], writes=[bb_])
                P.dma("pool", w16[:, :, DFF + j0 * 128:DFF + j1 * 128], wv[:, :, DFF + j0 * 128:DFF + j1 * 128],
                      writes=[bb_])

            issue_block(0)
            issue_block(1)
            issue_block(2)
            next_block = [3]
            wd16, bwd, wdsrc = pre
            wdv = wdsrc.rearrange("(j p) n -> p j n", p=128)
            wd_issued = [False]

            def issue_wd():
                if not wd_issued[0]:
                    wd_issued[0] = True
                    for j0 in range(0, 22, 6):
                        j1 = min(22, j0 + 6)
                        P.dma("pool", wd16[:, j0:j1, :], wdv[:, j0:j1, :], writes=[bwd])
            A_l = S.sb("A_l", [128, D], F32); B_l = S.sb("B_l", [128, D], F32)
            bAB = Buf("AB")
            self.load_bc(A_l, self.modv[0, 4096:5120], bAB)
            self.load_bc(B_l, self.modv[0, 3072:4096], bAB)
            xr = Ring(S.sb, "xt", 2, [128, D], F32)
            tr = Ring(S.sb, "t32", 2, [128, D], F32)
            hr = Ring(S.sb, "h16", 2, [128, D], BF16)
            hTr = Ring(S.sb, "hT", 2, [128, 8, 512], BF16)
            pTr = Ring(S.ps, "pT", 2, [128, 1024], BF16, excl=True)
            pg = Ring(S.ps, "pg", 3, [128, 512], F32, excl=True)
            pu = Ring(S.ps, "pu", 3, [128, 512], F32, excl=True)
            sr = Ring(S.sb, "sil", 3, [128, 512], F32)
            ar = Ring(S.sb, "a16", 4, [128, 512], BF16)
            chunks = [(ci * 512, 512, False) for ci in range(4)]
            if need_ctx:
                chunks.append((2048, 256, True))
            for (qoff, N, is_ctx) in chunks:
                hTt, hTb = hTr.next()
                if is_ctx:
                    self.load_bc(A_l, self.modv[1, 4096:5120], bAB)
                    self.load_bc(B_l, self.modv[1, 3072:4096], bAB)
                self.make_hT(S, dict(ap=self.XL1, deps=[bS["XL1"]]), qoff, N, A_l, B_l, bAB, xr, tr, hr, pTr, hTt, hTb)
                for j in range(22):
                    if next_block[0] < len(wblocks) and j % 2 == 1:
                        issue_block(next_block[0])
                        next_block[0] += 1
                    bw = bwj[j]
                    ptg, pbg = pg.next()
                    for k in range(8):
                        MM(P, ptg[:, 0:N], w16[:, k, j * 128:(j + 1) * 128], hTt[:, k, 0:N], k == 0, k == 7, [bw, hTb],
                           [pbg])
                    ptu, pbu = pu.next()
                    for k in range(8):
                        MM(P, ptu[:, 0:N], w16[:, k, DFF + j * 128: DFF + (j + 1) * 128], hTt[:, k, 0:N], k == 0, k == 7,
                           [bw, hTb], [pbu])
                    st, sb_ = sr.next()
                    ACTV(P, st[:, 0:N], ptg[:, 0:N], AF.Silu, [pbg], [sb_])
                    at, ab = ar.next()
                    TT(P, "dve", at[:, 0:N], ptu[:, 0:N], st[:, 0:N], ALU.mult, [pbu, sb_], [ab])
                    P.dma("act", self.ACTT[j, :, qoff:qoff + N], at[:, 0:N], reads=[ab], writes=[bA])
                issue_wd()

    def phase4b(self, l, need_ctx, y_out, ctx_out, wd16, bw):
        P, W = self.P, self.W[l]
        bS = self.b_scr
        with Scope(P) as S:
            g_l = S.sb("g_l", [128, D], F32); g_c = S.sb("g_c", [128, D], F32)
            lng = S.sb("lng", [128, D], F32); lnb = S.sb("lnb", [128, D], F32)
            bbc = Buf("bc4")
            self.load_bc(g_l, self.modv[0, 5120:6144], bbc)
            self.load_bc(g_c, self.modv[1, 5120:6144], bbc)
            P.dma("sp", lng[:], W["ln2_g"].partition_broadcast(128), writes=[bbc])
            P.dma("sp", lnb[:], W["ln2_b"].partition_broadcast(128), writes=[bbc])
            atr = Ring(S.sb, "aT", 2, [128, 22, 512], BF16)
            po = Ring(S.ps, "po", 6, [128, 512], F32, excl=True)
            rings = self.ln_rings(S)
            chunks = [(ci * 512, 512, g_l, y_out[0], ci * 512, y_out[1]) for ci in range(4)]
            if need_ctx:
                chunks.append((2048, 256, g_c, ctx_out[0], 0, ctx_out[1]))
            loaded = {}

            def load_chunk(ci_):
                q_, n_ = chunks[ci_][0], chunks[ci_][1]
                at_, ab_ = atr.next()
                P.dma("sp", at_[:, :, 0:n_], self.ACTT[:, :, q_:q_ + n_].rearrange("j p t -> p j t"),
                      reads=[bS["ACTT"]], writes=[ab_])
                loaded[ci_] = (at_, ab_)

            load_chunk(0)
            for ci_, (qoff, N, gbc, dst, drow, dbuf) in enumerate(chunks):
                at, ab = loaded[ci_]
                if ci_ + 1 < len(chunks):
                    load_chunk(ci_ + 1)
                for tt in range(N // 128):
                    halves = []
                    for nh in range(2):
                        pt, pb = po.next()
                        for j in range(22):
                            MM(P, pt[:, 0:512], at[:, j, tt * 128:(tt + 1) * 128], wd16[:, j, nh * 512:(nh + 1) * 512],
                               j == 0, j == 21, [ab, bw], [pb])
                        halves.append((pt, pb))
                    r0 = qoff + tt * 128
                    self.resid_ln(S, halves, (self.XL1[r0:r0 + 128, :], [bS["XL1"]]), gbc, lng, lnb, bbc,
                                  dst[drow + tt * 128: drow + (tt + 1) * 128, :], dbuf, rings)


def _rot_matrix(Dr):
    n_f = Dr // 4
    R = np.zeros((Dr, Dr), np.float32)
    for a in range(2):
        for f in range(n_f):
            i1 = a * 2 * n_f + f
            i2 = i1 + n_f
            R[i2, i1] = -1.0
            R[i1, i2] = 1.0
    return R


def _consts():
    c = np.zeros((5, 128, 128), np.float32)
    c[0] = np.eye(128, dtype=np.float32)
    r32, r64 = _rot_matrix(32), _rot_matrix(64)
    for i in range(4):
        c[1, i * 32:(i + 1) * 32, i * 32:(i + 1) * 32] = r32
    for i in range(2):
        c[2, i * 64:(i + 1) * 64, i * 64:(i + 1) * 64] = r64
        c[4, i * 64:(i + 1) * 64, i * 64:(i + 1) * 64] = 1.0
    c[3] = 1.0
    return c


def _rope_tables(half):
    out = np.zeros((4, 128, TK), np.float32)
    out[0, :, 4096:] = 1.0
    out[2, :, 4096:] = 1.0
    tg = np.concatenate([np.arange(2048) + half * 2048, np.arange(2048) + (1 - half) * 2048])
    pos = np.stack([tg // 64, tg % 64], axis=0).astype(np.float32)
    for ti, Dr in ((0, 32), (2, 64)):
        n_f = Dr // 4
        inv = (np.float32(10000.0) ** (-np.arange(n_f, dtype=np.float32) / np.float32(n_f))).astype(np.float32)
        p = np.arange(128)
        i = p % Dr
        a = i // (2 * n_f)
        f = i % n_f
        ang = (pos[a, :] * inv[f][:, None]).astype(np.float32)
        out[ti, :, :4096] = np.cos(ang)
        out[ti + 1, :, :4096] = np.sin(ang)
    return out


def _na_bias(rpb_l, half):
    out = np.empty((3, 8, 4, 128, 512), np.float32)
    kk = np.arange(128)
    qq = np.arange(512)
    for st, j in ((0, 0), (1, 1), (2, 3)):
        qr = 8 * j + qq // 64 + 32 * half
        qc = qq % 64
        rs = np.clip(qr - 4, 0, 56)
        cs = np.clip(qc - 8, 0, 48)
        for s in range(8):
            lt = 4 * j - 2 + s
            if lt < 0:
                lt += 32
            if lt < 16:
                kr = 2 * lt + kk // 64 + 32 * half
            else:
                kr = 2 * (lt - 16) + kk // 64 + 32 * (1 - half)
            kc = kk % 64
            inw = ((kr[:, None] >= rs[None]) & (kr[:, None] < rs[None] + 8) &
                   (kc[:, None] >= cs[None]) & (kc[:, None] < cs[None] + 16))
            ro = np.clip(kr[:, None] - qr[None] + 7, 0, 14)
            co = np.clip(kc[:, None] - qc[None] + 15, 0, 30)
            for h in range(4):
                out[st, s, h] = np.where(inw, rpb_l[h][ro, co], np.float32(NEG))
    return out


def _layer_inputs(inp, l, nab_by_half):
    f = lambda a: np.ascontiguousarray(a, dtype=np.float32)
    w_in = inp["w_in"][l]
    qd = w_in[:, 2208:2464].reshape(D, 4, 64)[:, [0, 2, 1, 3], :].reshape(D, 256)
    w_qkv = np.concatenate([w_in[:, :2208], qd, w_in[:, 2464:2720]], axis=1)
    wkv = inp["w_kv_up"][l].reshape(256, 4, 2, 64)
    wkv = np.concatenate([wkv[:, :, 0, :].reshape(256, 256), wkv[:, :, 1, :].reshape(256, 256)], axis=1)
    wb = np.array(inp["w_branch"][l], dtype=np.float32)
    wb[3] = wb[3].reshape(4, 64, D)[[0, 2, 1, 3]].reshape(256, D)
    s = "_%d" % l
    common = {
        "w_ada" + s: f(inp["w_ada"][l]), "b_ada" + s: f(inp["b_ada"][l]),
        "w_qkv" + s: f(w_qkv), "w_gate" + s: f(w_in[:, 2720:]),
        "gqaT" + s: f(inp["g_q_a"][l].reshape(3, 128).T), "w_q_up" + s: f(inp["w_q_up"][l]),
        "gkvT" + s: f(inp["g_kv_a"][l].reshape(2, 128).T), "w_kv_up" + s: f(wkv),
        "lamv" + s: f(np.stack([inp["lam_q1"][l].ravel(), inp["lam_k1"][l].ravel(),
                                inp["lam_q2"][l].ravel(), inp["lam_k2"][l].ravel()])),
        "gcols" + s: f(np.stack([np.tile(inp["g_sub"][l], 2), np.tile(inp["g_qn"][l], 2),
                                 np.tile(inp["g_kn"][l], 2)], axis=1)),
        "w_branch" + s: f(wb), "w_out" + s: f(inp["w_out"][l]),
        "ln1_g" + s: f(inp["ln1_g"][l]), "ln1_b" + s: f(inp["ln1_b"][l]),
        "w_gu" + s: f(inp["w_gate_up"][l]), "w_down" + s: f(inp["w_down"][l]),
        "ln2_g" + s: f(inp["ln2_g"][l]), "ln2_b" + s: f(inp["ln2_b"][l]),
    }
    per_half = []
    for half in range(2):
        d = dict(common)
        d["nab" + s] = nab_by_half[half]
        per_half.append(d)
    return per_half


_NC_CACHE = {}


def _get_nc(layers):
    key = tuple(layers)
    if key not in _NC_CACHE:
        _NC_CACHE[key] = Builder(layers).build()
    return _NC_CACHE[key]


def kernel(**inputs):
    inp = {k: np.asarray(v) for k, v in inputs.items()}
    x, c, ctx, c_ctx = inp["x"], inp["c"], inp["ctx"], inp["c_ctx"]
    consts = _consts()
    ropes = [_rope_tables(0), _rope_tables(1)]
    lays = []
    for l in range(DEPTH):
        nab = [_na_bias(inp["rpb"][l], 0), _na_bias(inp["rpb"][l], 1)]
        lays.append(_layer_inputs(inp, l, nab))
    sel = [np.zeros((128, D), np.uint32), np.ones((128, D), np.uint32)]
    in_maps = []
    for core in range(8):
        b, half = core // 2, core % 2
        m = {}
        for l in range(DEPTH):
            m.update(lays[l][half])
        m["x_own"] = np.ascontiguousarray(x[b, half * 2048:(half + 1) * 2048], dtype=np.float32)
        m["x_oth"] = np.ascontiguousarray(x[b, (1 - half) * 2048:(2 - half) * 2048], dtype=np.float32)
        m["ctx_in"] = np.ascontiguousarray(ctx[b], dtype=np.float32)
        cT = np.stack([c[b].reshape(8, 128).T, c_ctx.reshape(8, 128).T], axis=2).reshape(128, 16)
        m["cT"] = np.ascontiguousarray(cT, dtype=np.float32)
        m["consts"] = consts
        m["rope"] = ropes[half]
        m["selmask"] = sel[half]
        in_maps.append(m)
    nc = _get_nc(list(range(DEPTH)))
    res = run_bass_kernel_spmd(nc, in_maps, core_ids=list(range(8)))
    out = np.empty((4, 4096, D), np.float32)
    for core in range(8):
        out[core // 2, (core % 2) * 2048:(core % 2 + 1) * 2048] = np.asarray(res.results[core]["y_out"])
    return out
```

```python
import math
import numpy as np
from contextlib import ExitStack
import concourse.bass as bass
import concourse.mybir as mybir
from concourse.bass_utils import run_bass_kernel_spmd

F32 = mybir.dt.float32
BF16 = mybir.dt.bfloat16
AF = mybir.ActivationFunctionType
ALU = mybir.AluOpType

SEM_LIMIT = 30000
DMA_SLOTS = 16

D = 1024
DEPTH = 2
TK = 4352
TQ = 2304
DFF = 2816
EPS = 1e-6
ALPHA = (2 * DEPTH) ** 0.25
NEG = -30000.0
CQ, CKV, KPE, QB, KB, VB, QC, KC, VC, QD, KD, VD = 0, 384, 640, 672, 928, 1184, 1440, 1696, 1952, 2208, 2464, 2592
NQKV = 2720


class Buf:
    __slots__ = ("name", "w", "r", "excl")

    def __init__(self, name, excl=False):
        self.name = name
        self.w = None
        self.r = {}
        self.excl = excl


class Prog:
    ENGS = ("pe", "act", "dve", "pool", "sp")

    def __init__(self, nc, es):
        self.nc = nc
        self.es = es
        self.sems = []
        self.q = {e: [] for e in self.ENGS}
        self.cur = {}
        self.cnt = {}
        for e in self.ENGS:
            self.cur[e] = self._new_sem("c_" + e)
            self.cnt[e] = 0
        self.seen = {e: {} for e in self.ENGS}
        self.dma_sem = {}
        self.dma_i = {}
        self.last = {}

    def _new_sem(self, name):
        h = self.es.enter_context(self.nc.semaphore(name + "_%d" % len(self.sems)))
        self.sems.append(h)
        return len(self.sems) - 1

    def _deps(self, reads, writes):
        waits = {}
        for b in reads:
            if b.w is not None and waits.get(b.w[0], 0) < b.w[1]:
                waits[b.w[0]] = b.w[1]
            if b.excl:
                for k, v in b.r.items():
                    if waits.get(k, 0) < v:
                        waits[k] = v
        for b in writes:
            if b.w is not None and waits.get(b.w[0], 0) < b.w[1]:
                waits[b.w[0]] = b.w[1]
            for k, v in b.r.items():
                if waits.get(k, 0) < v:
                    waits[k] = v
        return waits

    def _prune(self, eng, waits, skip_own=False):
        seen = self.seen[eng]
        wl = []
        for k, v in waits.items():
            if k == self.cur.get(eng):
                if skip_own or v < self.cnt[eng] - 1:
                    continue
            if seen.get(k, 0) >= v:
                continue
            seen[k] = v
            wl.append((k, v))
        return wl

    def _commit(self, ev, reads, writes):
        k, v = ev
        self.last[k] = v
        for b in writes:
            b.w = ev
            b.r = {}
        for b in reads:
            if b in writes:
                continue
            if b.excl:
                b.w = ev
                b.r = {}
            elif b.r.get(k, 0) < v:
                b.r[k] = v

    def op(self, eng, fn, reads=(), writes=()):
        waits = self._deps(reads, writes)
        wl = self._prune(eng, waits, skip_own=(eng == "pe"))
        if self.cnt[eng] >= SEM_LIMIT:
            self.cur[eng] = self._new_sem("c_" + eng)
            self.cnt[eng] = 0
        self.cnt[eng] += 1
        ev = (self.cur[eng], self.cnt[eng])
        self.q[eng].append((wl, fn, ev[0], 1))
        self._commit(ev, reads, writes)
        return ev

    def dma(self, queue, out, in_, reads=(), writes=(), fn=None):
        waits = self._deps(reads, writes)
        i = self.dma_i.get(queue, 0)
        self.dma_i[queue] = i + 1
        slot = i % DMA_SLOTS
        rnd = i // DMA_SLOTS
        per_gen = SEM_LIMIT // 16 - 2
        gen = rnd // per_gen
        kk = rnd % per_gen
        key = (queue, slot, gen)
        if key not in self.dma_sem:
            self.dma_sem[key] = self._new_sem("d_%s%d" % (queue, slot))
        sk = self.dma_sem[key]
        if kk > 0:
            if waits.get(sk, 0) < 16 * kk:
                waits[sk] = 16 * kk
        elif gen > 0:
            pk = self.dma_sem[(queue, slot, gen - 1)]
            if waits.get(pk, 0) < 16 * per_gen:
                waits[pk] = 16 * per_gen
        wl = self._prune(queue, waits)
        ev = (sk, 16 * (kk + 1))
        if fn is None:
            fn = (lambda e: e.dma_start(out=out, in_=in_))
        self.q[queue].append((wl, fn, sk, 16))
        self._commit(ev, reads, writes)
        return ev

    def barrier(self):
        evs = dict(self.last)
        for e in self.ENGS:
            wl = self._prune(e, dict(evs))
            if wl:
                self.q[e].append((wl, None, None, 0))

    def emit(self):
        nc = self.nc
        sems = self.sems
        q = self.q

        def run(e, lst):
            for wl, fn, sk, inc in lst:
                for k, v in wl:
                    e.wait_ge(sems[k], v)
                if fn is not None:
                    fn(e).then_inc(sems[sk], inc)

        with nc.Block() as block:
            @block.tensor
            def _(e):
                run(e, q["pe"])

            @block.scalar
            def _(e):
                run(e, q["act"])

            @block.vector
            def _(e):
                run(e, q["dve"])

            @block.gpsimd
            def _(e):
                run(e, q["pool"])

            @block.sync
            def _(e):
                run(e, q["sp"])


class Ring:
    def __init__(self, alloc, name, n, shape, dtype, excl=False):
        self.t = [alloc(name + str(i), shape, dtype) for i in range(n)]
        self.b = [Buf(name + str(i), excl) for i in range(n)]
        self.i = 0

    def next(self):
        j = self.i % len(self.t)
        self.i += 1
        return self.t[j], self.b[j]


class Scope:
    def __init__(self, P):
        self.P = P
        self.es = ExitStack()

    def __enter__(self):
        self.es.__enter__()
        return self

    def __exit__(self, *a):
        self.P.barrier()
        return self.es.__exit__(*a)

    _uid = [0]

    def sb(self, name, shape, dtype):
        Scope._uid[0] += 1
        return self.es.enter_context(self.P.nc.sbuf_tensor("%s_u%d" % (name, Scope._uid[0]), list(shape), dtype))

    def ps(self, name, shape, dtype):
        Scope._uid[0] += 1
        return self.es.enter_context(self.P.nc.psum_tensor("%s_u%d" % (name, Scope._uid[0]), list(shape), dtype))


def MM(P, out, lhsT, rhs, start, stop, r, w, **kw):
    return P.op("pe", lambda e: e.matmul(out, lhsT=lhsT, rhs=rhs, start=start, stop=stop, **kw), r, w)


def TR(P, out, in_, ident, r, w):
    return P.op("pe", lambda e: e.transpose(out=out, in_=in_, identity=ident), r, w)


def ACTV(P, out, in_, func, r, w, bias=None, scale=None):
    kw = {}
    if bias is not None:
        kw["bias"] = bias
    if scale is not None:
        kw["scale"] = scale
    return P.op("act", lambda e: e.activation(out=out, in_=in_, func=func, **kw), r, w)


def TT(P, eng, out, in0, in1, op, r, w):
    return P.op(eng, lambda e: e.tensor_tensor(out=out, in0=in0, in1=in1, op=op), r, w)


def TS(P, eng, out, in0, s1, s2, op0, op1, r, w):
    if s2 is None:
        return P.op(eng, lambda e: e.tensor_scalar(out=out, in0=in0, scalar1=s1, scalar2=None, op0=op0), r, w)
    return P.op(eng, lambda e: e.tensor_scalar(out=out, in0=in0, scalar1=s1, scalar2=s2, op0=op0, op1=op1), r, w)


def STT(P, eng, out, in0, scalar, in1, op0, op1, r, w):
    return P.op(eng, lambda e: e.scalar_tensor_tensor(out=out, in0=in0, scalar=scalar, in1=in1, op0=op0, op1=op1), r, w)


def CP(P, eng, out, in_, r, w):
    if eng == "act":
        return P.op("act", lambda e: e.copy(out=out, in_=in_), r, w)
    return P.op(eng, lambda e: e.tensor_copy(out=out, in_=in_), r, w)


def RECIP(P, out, in_, r, w):
    return P.op("dve", lambda e: e.reciprocal(out=out, in_=in_), r, w)


def MEMSET(P, eng, ap, val, r, w):
    return P.op(eng, lambda e: e.memset(ap, val), r, w)


class Builder:
    def __init__(self, layers):
        self.layers = list(layers)
        self.nc = bass.Bass("TRN2", target_bir_lowering=False)
        self.es = ExitStack()

    def din(self, name, shape, dtype=F32):
        return self.nc.dram_tensor(name, list(shape), dtype, kind="ExternalInput").ap()

    def dout(self, name, shape, dtype=F32):
        return self.nc.dram_tensor(name, list(shape), dtype, kind="ExternalOutput").ap()

    def dscr(self, name, shape, dtype=F32):
        return self.nc.dram_tensor(name, list(shape), dtype).ap()

    def build(self):
        nc = self.nc
        with self.es:
            P = self.P = Prog(nc, self.es)
            self.x_own = self.din("x_own", [2048, D])
            self.x_oth = self.din("x_oth", [2048, D])
            self.ctx_in = self.din("ctx_in", [256, D])
            self.cT = self.din("cT", [128, 16])
            self.consts_d = self.din("consts", [5, 128, 128])
            self.rope_d = self.din("rope", [4, 128, TK])
            self.W = {}
            for l in self.layers:
                w = {}
                s = "_%d" % l
                w["w_ada"] = self.din("w_ada" + s, [D, 6 * D])
                w["b_ada"] = self.din("b_ada" + s, [6 * D])
                w["w_qkv"] = self.din("w_qkv" + s, [D, NQKV])
                w["w_gate"] = self.din("w_gate" + s, [D, 4 * D])
                w["gqaT"] = self.din("gqaT" + s, [128, 3])
                w["w_q_up"] = self.din("w_q_up" + s, [384, 384])
                w["gkvT"] = self.din("gkvT" + s, [128, 2])
                w["w_kv_up"] = self.din("w_kv_up" + s, [256, 512])
                w["lamv"] = self.din("lamv" + s, [4, 128])
                w["gcols"] = self.din("gcols" + s, [128, 3])
                w["w_branch"] = self.din("w_branch" + s, [4, 256, D])
                w["w_out"] = self.din("w_out" + s, [D, D])
                w["ln1_g"] = self.din("ln1_g" + s, [D])
                w["ln1_b"] = self.din("ln1_b" + s, [D])
                w["w_gu"] = self.din("w_gu" + s, [D, 2 * DFF])
                w["w_down"] = self.din("w_down" + s, [DFF, D])
                w["ln2_g"] = self.din("ln2_g" + s, [D])
                w["ln2_b"] = self.din("ln2_b" + s, [D])
                w["nab"] = self.din("nab" + s, [3, 8, 4, 128, 512])
                self.W[l] = w
            self.y_out = self.dout("y_out", [2048, D])
            self.fused = (len(self.layers) == 2)
            if not self.fused:
                self.ctx_out = self.dout("ctx_out", [256, D])
            else:
                self.selmask_d = self.din("selmask", [128, D], mybir.dt.uint32)
                self.XO = self.dscr("XO", [2048, D])
                self.CO = self.dscr("CO", [256, D])
            self.modv = self.dscr("modv", [2, 6 * D])
            self.QA = self.dscr("QA", [4, 96, TQ], BF16)
            self.KA = self.dscr("KA", [4, 96, TK], BF16)
            self.VA = self.dscr("VA", [TK, 256], BF16)
            self.QBs = self.dscr("QBs", [256, TQ], BF16)
            self.KBs = self.dscr("KBs", [256, TK], BF16)
            self.VBs = self.dscr("VBs", [TK, 256], BF16)
            self.QCs = self.dscr("QCs", [256, TQ], BF16)
            self.KCs = self.dscr("KCs", [256, TK], BF16)
            self.VCs = self.dscr("VCs", [TK, 256], BF16)
            self.QDs = self.dscr("QDs", [256, TQ], BF16)
            self.KDs = self.dscr("KDs", [128, TK], BF16)
            self.VDs = self.dscr("VDs", [TK, 128], BF16)
            self.G = self.dscr("G", [32, 128, TQ])
            self.XL1 = self.dscr("XL1", [TQ, D])
            self.ACTT = self.dscr("ACTT", [22, 128, TQ], BF16)
            self.b_scr = {}
            self.b_yout = Buf('yout'); self.b_cout = Buf('cout')

            sbp = lambda n, s, d: self.es.enter_context(nc.sbuf_tensor(n, list(s), d))
            self.ident = sbp("ident", [128, 128], BF16)
            self.R32 = sbp("R32", [128, 128], BF16)
            self.R64 = sbp("R64", [128, 128], BF16)
            self.ones_f = sbp("ones_f", [128, 128], F32)
            self.blk_f = sbp("blk_f", [128, 128], F32)
            self.eps_col = sbp("eps_col", [128, 1], F32)
            self.b_const = Buf("const")
            P.dma("pool", self.ident[:], self.consts_d[0], writes=[self.b_const])
            P.dma("pool", self.R32[:], self.consts_d[1], writes=[self.b_const])
            P.dma("pool", self.R64[:], self.consts_d[2], writes=[self.b_const])
            P.dma("sp", self.ones_f[:], self.consts_d[3], writes=[self.b_const])
            P.dma("sp", self.blk_f[:], self.consts_d[4], writes=[self.b_const])
            MEMSET(P, "dve", self.eps_col[:], EPS, [], [self.b_const])
            self.wq16 = sbp("wq16", [128, 3, 416], BF16)
            self.wkv16 = sbp("wkv16", [128, 2, 512], BF16)
            self.lamt = sbp("lamt", [128, 8], F32)
            self.gcols = sbp("gcols", [128, 3], F32)
            self.b_small = Buf("small")
            MEMSET(P, "pool", self.wq16[:], 0.0, [], [self.b_small])

            ext = lambda ap: dict(ap=ap, deps=[])
            if not self.fused:
                l = self.layers[0]
                self.layer(l, l < DEPTH - 1, ext(self.x_own), ext(self.x_oth), ext(self.ctx_in),
                           (self.y_out, self.b_yout), (self.ctx_out, self.b_cout))
            else:
                self.selmask = sbp("selmask_sb", [128, D], mybir.dt.uint32)
                P.dma("sp", self.selmask[:], self.selmask_d, writes=[self.b_const])
                bXO, bCO, bXG = Buf("XO"), Buf("CO"), Buf("XG")
                self.layer(0, True, ext(self.x_own), ext(self.x_oth), ext(self.ctx_in), (self.XO, bXO), (self.CO, bCO))
                XO = self.XO
                XGs = [self.dscr("XG%d" % c_, [512, D]) for c_ in range(8)]
                for c_ in range(8):
                    P.op("pool", lambda e, c_=c_: e.collective_compute(
                        "AllGather", ALU.bypass, replica_groups=[[0, 1], [2, 3], [4, 5], [6, 7]],
                        ins=[XO[c_ * 256:(c_ + 1) * 256, :]], outs=[XGs[c_]]), [bXO], [bXG])

                def get_oth(r0):
                    c_, r = r0 // 256, r0 % 256
                    return XGs[c_][256 + r:256 + r + 128, :], XGs[c_][r:r + 128, :]
                self.layer(1, False, dict(ap=self.XO, deps=[bXO]),
                           dict(get=get_oth, deps=[bXG]),
                           dict(ap=self.CO, deps=[bCO]), (self.y_out, self.b_yout), None)
            P.barrier()
            P.emit()
        return nc

    def layer(self, l, need_ctx, x_own, x_oth, ctx_in, y_out, ctx_out):
        P = self.P
        W = self.W[l]
        with Scope(P) as GS:
            wq16_ = GS.sb("wqkv16", [128, 8, NQKV], BF16)
            bwq = Buf("wqkv")
            wv_ = W["w_qkv"].rearrange("(k p) n -> p k n", p=128)
            for k in range(8):
                P.dma("pool", wq16_[:, k, :], wv_[:, k, :], writes=[bwq])
            self.phase0(l)
            with Scope(P) as GS2:
                wg16 = GS2.sb("wg16", [128, 8, 4 * D], BF16)
                bwg = Buf("wg")
                self.phase1a(l, need_ctx, x_own, x_oth, ctx_in, (wg16, bwg, W["w_gate"]), wq16_, bwq)
                self.phase1b(l, need_ctx, x_own, ctx_in, wg16, bwg)
        with Scope(P) as YS:
            yT = YS.sb("yT", [128, 8, TQ], BF16)
            byT = Buf("yT")
            wb16 = YS.sb("wb16", [128, 8, D], BF16)
            wo16 = YS.sb("wo16", [128, 8, D], BF16)
            bwbo = Buf("wbo")
            P.dma("pool", wb16[:], W["w_branch"].rearrange("i (k p) n -> p (i k) n", p=128), writes=[bwbo])
            P.dma("pool", wo16[:], W["w_out"].rearrange("(k p) n -> p k n", p=128), writes=[bwbo])
            self.phase2(l, need_ctx, yT, byT)
            self.phase3(l, need_ctx, x_own, ctx_in, yT, byT, wb16, wo16, bwbo)
        with Scope(P) as DS:
            wd16 = DS.sb("wd16", [128, 22, D], BF16)
            bwd = Buf("wd")
            self.phase4a(l, need_ctx, (wd16, bwd, W["w_down"]))
            self.phase4b(l, need_ctx, y_out, ctx_out, wd16, bwd)

    def phase0(self, l):
        P, W = self.P, self.W[l]
        lam_init = 0.8 - 0.6 * math.exp(-0.3 * l)
        bmod = self.b_scr.setdefault("modv", Buf("modv"))
        with Scope(P) as S:
            sT = S.sb("sT", [128, 16], F32)
            bsT = Buf("sT")
            P.dma("sp", sT[:], self.cT, writes=[bsT])
            ACTV(P, sT[:], sT[:], AF.Silu, [bsT], [bsT])
            bb = S.sb("bb", [2, 6 * D], F32)
            bbb = Buf("bb")
            P.dma("sp", bb[:], W["b_ada"].partition_broadcast(2), writes=[bbb])
            mod = S.sb("mod", [2, 6 * D], F32)
            bm = Buf("mod")
            wr = Ring(S.sb, "wada", 2, [128, 8, 512], F32)
            pr = Ring(S.ps, "p0", 2, [128, 512], F32, excl=True)
            wv = W["w_ada"].rearrange("(k p) n -> p k n", p=128)
            sTv = sT[:].rearrange("p (k r) -> p k r", r=2)
            for n in range(12):
                wt, wb = wr.next()
                P.dma("sp", wt[:], wv[:, :, n * 512:(n + 1) * 512], writes=[wb])
                pt, pb = pr.next()
                for k in range(8):
                    MM(P, pt[0:2, :], sTv[:, k, :], wt[:, k, :], k == 0, k == 7, [bsT, wb], [pb])
                TT(P, "dve", mod[:, n * 512:(n + 1) * 512], pt[0:2, :], bb[:, n * 512:(n + 1) * 512], ALU.add,
                   [pb, bbb], [bm])
            TS(P, "dve", mod[:, 1024:2048], mod[:, 1024:2048], 1.0, None, ALU.add, None, [bm], [bm])
            TS(P, "dve", mod[:, 4096:5120], mod[:, 4096:5120], 1.0, None, ALU.add, None, [bm], [bm])
            P.dma("sp", self.modv, mod[:], reads=[bm], writes=[bmod])
            lv = S.sb("lv", [128, 4, 128], F32)
            blv = Buf("lv")
            for i in range(4):
                P.dma("sp", lv[:, i, :], W["lamv"][i].partition_broadcast(128), writes=[blv])
            pr1 = S.sb("pr1", [128, 2, 128], F32)
            bp1 = Buf("pr1")
            TT(P, "dve", pr1[:, 0, :], lv[:, 0, :], lv[:, 1, :], ALU.mult, [blv], [bp1])
            TT(P, "dve", pr1[:, 1, :], lv[:, 2, :], lv[:, 3, :], ALU.mult, [blv], [bp1])
            bs = self.b_small
            P.op("dve", lambda e: e.reduce_sum(out=self.lamt[:, 0:8], in_=pr1[:].rearrange("p a (h d) -> p (a h) d", d=32),
                                               axis=mybir.AxisListType.X), [bp1], [bs])
            ACTV(P, self.lamt[:, 0:8], self.lamt[:, 0:8], AF.Exp, [bs], [bs])
            STT(P, "dve", self.lamt[:, 0:4], self.lamt[:, 4:8], -lam_init, self.lamt[:, 0:4], ALU.add, ALU.subtract,
                [bs], [bs])
            P.dma("sp", self.gcols[:], W["gcols"], writes=[bs])
            TS(P, "dve", self.gcols[:, 0:1], self.gcols[:, 0:1], 1.0 - lam_init, None, ALU.mult, None, [bs], [bs])
            wq32 = S.sb("wq32", [128, 3, 384], F32)
            wkv32 = S.sb("wkv32", [128, 2, 512], F32)
            gq = S.sb("gq", [128, 3], F32)
            gkv = S.sb("gkv", [128, 2], F32)
            bw = Buf("wup")
            P.dma("sp", wq32[:], W["w_q_up"].rearrange("(k p) n -> p k n", p=128), writes=[bw])
            P.dma("sp", wkv32[:], W["w_kv_up"].rearrange("(k p) n -> p k n", p=128), writes=[bw])
            P.dma("sp", gq[:], W["gqaT"], writes=[bw])
            P.dma("sp", gkv[:], W["gkvT"], writes=[bw])
            for k in range(3):
                TS(P, "dve", self.wq16[:, k, 0:384], wq32[:, k, :], gq[:, k:k + 1], None, ALU.mult, None, [bw], [bs])
            for k in range(2):
                TS(P, "dve", self.wkv16[:, k, :], wkv32[:, k, :], gkv[:, k:k + 1], None, ALU.mult, None, [bw], [bs])

    def load_bc(self, t, src_row, b):
        self.P.dma("sp", t[:], src_row.partition_broadcast(128), reads=[self.b_scr["modv"]], writes=[b])

    def make_hT(self, S, xsrc, row0, N, A, Bc, bAB, xr, tr, hr, pTr, hTt, hTb):
        P = self.P
        for tt in range(N // 128):
            xt, xb = xr.next()
            r0 = row0 + tt * 128
            if "get" in xsrc:
                ap1, ap2 = xsrc["get"](r0)
            else:
                ap1, ap2 = xsrc["ap"][r0:r0 + 128, :], None
            P.dma("sp", xt[:], ap1, reads=xsrc["deps"], writes=[xb])
            if ap2 is not None:
                x2, x2b = xr.next()
                P.dma("sp", x2[:], ap2, reads=xsrc["deps"], writes=[x2b])
                P.op("dve", lambda e, xt=xt, x2=x2: e.copy_predicated(xt[:], self.selmask[:], x2[:]),
                     [x2b, self.b_const], [xb])
            t32, tb = tr.next()
            TT(P, "dve", t32[:], xt[:], A[:], ALU.mult, [xb, bAB], [tb])
            h16, hb = hr.next()
            TT(P, "dve", h16[:], t32[:], Bc[:], ALU.add, [tb, bAB], [hb])
            pT, pb = pTr.next()
            for k in range(8):
                TR(P, pT[:, k * 128:(k + 1) * 128], h16[:, k * 128:(k + 1) * 128], self.ident[:],
                   [hb, self.b_const], [pb])
            eng = "act" if tt % 2 == 0 else "dve"
            CP(P, eng, hTt[:, :, tt * 128:(tt + 1) * 128], pT[:].rearrange("p (k t) -> p k t", t=128), [pb], [hTb])

    def phase1a(self, l, need_ctx, x_own, x_oth, ctx_in, pre, w16, bw):
        P, W = self.P, self.W[l]
        bsm, bc = self.b_small, self.b_const
        bS = self.b_scr
        for nm in ("QA", "KA", "VA", "QB", "KB", "VB", "QC", "KC", "VC", "QD", "KD", "VD"):
            bS.setdefault(nm, Buf(nm))
        with Scope(P) as S:
            pw16, pbw, pwsrc = pre
            pwv = pwsrc.rearrange("(k p) n -> p k n", p=128)
            pre_k = [0]

            def prefetch_step():
                if pre_k[0] < 8:
                    k_ = pre_k[0]
                    pre_k[0] += 1
                    P.dma("pool", pw16[:, k_, :], pwv[:, k_, :], writes=[pbw])
            A_l = S.sb("A_l", [128, D], F32); B_l = S.sb("B_l", [128, D], F32)
            bAB = Buf("AB")
            self.load_bc(A_l, self.modv[0, 1024:2048], bAB)
            self.load_bc(B_l, self.modv[0, 0:1024], bAB)
            xr = Ring(S.sb, "xt", 3, [128, D], F32)
            tr = Ring(S.sb, "t32", 2, [128, D], F32)
            hr = Ring(S.sb, "h16", 2, [128, D], BF16)
            hTr = Ring(S.sb, "hT", 2, [128, 8, 512], BF16)
            pTr = Ring(S.ps, "pT", 2, [128, 1024], BF16, excl=True)
            pp = Ring(S.ps, "pp", 4, [128, 512], F32, excl=True)
            pm = Ring(S.ps, "pm", 2, [128, 512], F32, excl=True)
            rope_r = Ring(S.sb, "rope", 1, [128, 4, 512], F32)
            sq_r = Ring(S.sb, "sq", 1, [128, 3, 512], F32)
            cq_r = Ring(S.sb, "cqT", 1, [128, 3, 512], BF16)
            ckv_r = Ring(S.sb, "ckvT", 1, [128, 2, 512], BF16)
            rstd_r = Ring(S.sb, "rstd", 2, [128, 512], F32)
            o16 = Ring(S.sb, "o16", 8, [128, 512], BF16)
            f32r = Ring(S.sb, "f32t", 4, [128, 512], F32)
            rc_r = Ring(S.sb, "rcol", 4, [128, 1], F32)
            for t_, b_ in zip(o16.t, o16.b):
                MEMSET(P, "pool", t_[:], 0.0, [], [b_])

            chunks = []
            for ci in range(4):
                chunks.append((x_own, ci * 512, 512, ci * 512, ci * 512, False))
            for ci in range(4):
                chunks.append((x_oth, ci * 512, 512, 2048 + ci * 512, None, False))
            chunks.append((ctx_in, 0, 256, 4096, 2048 if need_ctx else None, True))
            ev_i = [0]

            def evac_eng():
                ev_i[0] += 1
                return "act" if ev_i[0] % 2 == 0 else "dve"

            for (xsrc, row0, N, koff, qoff, is_ctx) in chunks:
                prefetch_step()
                hTt, hTb = hTr.next()
                if is_ctx:
                    self.load_bc(A_l, self.modv[1, 1024:2048], bAB)
                    self.load_bc(B_l, self.modv[1, 0:1024], bAB)
                self.make_hT(S, xsrc, row0, N, A_l, B_l, bAB, xr, tr, hr, pTr, hTt, hTb)
                rp, rpb_ = rope_r.next()
                for i in range(4):
                    P.dma("sp", rp[:, i, 0:N], self.rope_d[i, :, koff:koff + N], writes=[rpb_])
                cos32, sin32, cos64, sin64 = rp[:, 0, :], rp[:, 1, :], rp[:, 2, :], rp[:, 3, :]

                def proj_fm(c0, M):
                    pt, pb = pp.next()
                    for k in range(8):
                        MM(P, pt[0:M, 0:N], w16[:, k, c0:c0 + M], hTt[:, k, 0:N], k == 0, k == 7, [bw, hTb], [pb])
                    return pt, pb

                def proj_tm(tt, c0, ncol):
                    pt, pb = pp.next()
                    for k in range(8):
                        MM(P, pt[:, 0:ncol], hTt[:, k, tt * 128:(tt + 1) * 128], w16[:, k, c0:c0 + ncol], k == 0, k == 7,
                           [bw, hTb], [pb])
                    return pt, pb

                def rstd_rep(sq_t, sq_b, nk, lhs, inv_n):
                    pt, pb = pm.next()
                    for k in range(nk):
                        MM(P, pt[:, 0:N], lhs, sq_t[:, k, 0:N], k == 0, k == nk - 1, [sq_b, bc], [pb])
                    rt, rb = rstd_r.next()
                    ACTV(P, rt[:, 0:N], pt[:, 0:N], AF.Sqrt, [pb, bc], [rb], bias=self.eps_col[:], scale=inv_n)
                    RECIP(P, rt[:, 0:N], rt[:, 0:N], [rb], [rb])
                    return rt, rb

                def rope(src16, sb_, p0, p1, R, cosT, sinT, dst16, db_):
                    pt, pb = pm.next()
                    MM(P, pt[:, 0:N], R[:, :], src16[:, 0:N], True, True, [sb_, bc], [pb])
                    t1, b1 = f32r.next()
                    TT(P, "pool", t1[p0:p1, 0:N], src16[p0:p1, 0:N], cosT[p0:p1, 0:N], ALU.mult, [sb_, rpb_], [b1])
                    t2, b2 = f32r.next()
                    TT(P, "dve", t2[p0:p1, 0:N], pt[p0:p1, 0:N], sinT[p0:p1, 0:N], ALU.mult, [pb, rpb_], [b2])
                    TT(P, "pool", dst16[p0:p1, 0:N], t1[p0:p1, 0:N], t2[p0:p1, 0:N], ALU.add, [b1, b2], [db_])

                own = qoff is not None
                if own:
                    sqt, sqb = sq_r.next()
                    cqt, cqb = cq_r.next()
                    for k in range(3):
                        pt, pb = proj_fm(CQ + k * 128, 128)
                        CP(P, "dve", cqt[:, k, 0:N], pt[:, 0:N], [pb], [cqb])
                        ACTV(P, sqt[:, k, 0:N], pt[:, 0:N], AF.Square, [pb], [sqb])
                    rq, rqb = rstd_rep(sqt, sqb, 3, self.ones_f[:], 1.0 / 384)
                    for h in range(4):
                        pt, pb = pp.next()
                        for k in range(3):
                            MM(P, pt[:, 0:N], self.wq16[:, k, h * 96:h * 96 + 128], cqt[:, k, 0:N], k == 0, k == 2,
                               [bsm, cqb], [pb])
                        qh, qb_ = o16.next()
                        TT(P, "dve", qh[0:96, 0:N], pt[0:96, 0:N], rq[0:96, 0:N], ALU.mult, [pb, rqb], [qb_])
                        rope(qh, qb_, 64, 96, self.R32, cos32, sin32, qh, qb_)
                        P.dma("pool", self.QA[h, :, qoff:qoff + N], qh[0:96, 0:N], reads=[qb_], writes=[bS["QA"]])
                sqt, sqb = sq_r.next()
                ckt, ckb = ckv_r.next()
                for k in range(2):
                    pt, pb = proj_fm(CKV + k * 128, 128)
                    CP(P, "dve", ckt[:, k, 0:N], pt[:, 0:N], [pb], [ckb])
                    ACTV(P, sqt[:, k, 0:N], pt[:, 0:N], AF.Square, [pb], [sqb])
                rkv, rkvb = rstd_rep(sqt, sqb, 2, self.ones_f[:], 1.0 / 256)
                pt, pb = proj_fm(KPE - 64, 128)
                kp, kpb = o16.next()
                CP(P, "act", kp[64:96, 0:N], pt[64:96, 0:N], [pb], [kpb])
                kr, krb = o16.next()
                rope(kp, kpb, 64, 96, self.R32, cos32, sin32, kr, krb)
                for h in range(4):
                    P.dma("pool", self.KA[h, 64:96, koff:koff + N], kr[64:96, 0:N], reads=[krb], writes=[bS["KA"]])
                for h in range(4):
                    pt, pb = pp.next()
                    for k in range(2):
                        MM(P, pt[:, 0:N], self.wkv16[:, k, h * 64:h * 64 + 128], ckt[:, k, 0:N], k == 0, k == 1,
                           [bsm, ckb], [pb])
                    kn, knb = o16.next()
                    TT(P, "dve", kn[0:64, 0:N], pt[0:64, 0:N], rkv[0:64, 0:N], ALU.mult, [pb, rkvb], [knb])
                    P.dma("pool", self.KA[h, 0:64, koff:koff + N], kn[0:64, 0:N], reads=[knb], writes=[bS["KA"]])
                for tt in range(N // 128):
                    pt, pb = pp.next()
                    for k in range(2):
                        MM(P, pt[:, 0:256], ckt[:, k, tt * 128:(tt + 1) * 128], self.wkv16[:, k, 256:512], k == 0, k == 1,
                           [bsm, ckb], [pb])
                    pc, pcb = pm.next()
                    for k in range(2):
                        MM(P, pc[:, 0:1], sqt[:, k, tt * 128:(tt + 1) * 128], self.ones_f[:, 0:1], k == 0, k == 1,
                           [sqb, bc], [pcb])
                    rc, rcb = rc_r.next()
                    ACTV(P, rc[:], pc[:, 0:1], AF.Sqrt, [pcb, bc], [rcb], bias=self.eps_col[:], scale=1.0 / 256)
                    RECIP(P, rc[:], rc[:], [rcb], [rcb])
                    vt, vb = o16.next()
                    TS(P, "dve", vt[:, 0:256], pt[:, 0:256], rc[:, 0:1], None, ALU.mult, None, [pb, rcb], [vb])
                    P.dma("pool", self.VA[koff + tt * 128: koff + (tt + 1) * 128, :], vt[:, 0:256], reads=[vb],
                          writes=[bS["VA"]])
                for c in range(2):
                    if own:
                        pt, pb = proj_fm(QB + c * 128, 128)
                        ot, ob = o16.next()
                        eg = evac_eng()
                        CP(P, eg, ot[:, 0:N], pt[:, 0:N], [pb], [ob])
                        P.dma("act" if eg == "act" else "pool", self.QBs[c * 128:(c + 1) * 128, qoff:qoff + N], ot[:, 0:N], reads=[ob],
                              writes=[bS["QB"]])
                    pt, pb = proj_fm(KB + c * 128, 128)
                    ot, ob = o16.next()
                    eg = evac_eng()
                    CP(P, eg, ot[:, 0:N], pt[:, 0:N], [pb], [ob])
                    P.dma("act" if eg == "act" else "pool", self.KBs[c * 128:(c + 1) * 128, koff:koff + N], ot[:, 0:N], reads=[ob], writes=[bS["KB"]])
                later = []

                def diff_stage_b(s16, sb_, dst, c, off, nm):
                    def run():
                        d16, db_ = o16.next()
                        rope(s16, sb_, 0, 128, self.R32, cos32, sin32, d16, db_)
                        P.dma("pool", dst[c * 128:(c + 1) * 128, off:off + N], d16[:, 0:N], reads=[db_], writes=[bS[nm]])
                    return run

                for c in range(2):
                    for (col, dst, nm, doit, off) in ((QC, self.QCs, "QC", own, qoff), (KC, self.KCs, "KC", True, koff)):
                        if not doit:
                            continue
                        pt, pb = proj_fm(col + c * 128, 128)
                        s16, sb_ = o16.next()
                        CP(P, evac_eng(), s16[:, 0:N], pt[:, 0:N], [pb], [sb_])
                        later.append(diff_stage_b(s16, sb_, dst, c, off, nm))
                        if len(later) > 1:
                            later.pop(0)()
                specs = []
                if own:
                    specs += [(QD, self.QDs, "QD", 0, 1, qoff), (QD + 128, self.QDs, "QD", 1, 1, qoff)]
                specs.append((KD, self.KDs, "KD", 0, 2, koff))
                for (col, dst, nm, c, gi, off) in specs:
                    pt, pb = proj_fm(col, 128)
                    sq1, sq1b = sq_r.next()
                    ACTV(P, sq1[:, 0, 0:N], pt[:, 0:N], AF.Square, [pb], [sq1b])
                    rr, rrb = rstd_rep(sq1, sq1b, 1, self.blk_f[:], 1.0 / 64)
                    s16, sb_ = o16.next()
                    STT(P, "dve", s16[:, 0:N], pt[:, 0:N], self.gcols[:, gi:gi + 1], rr[:, 0:N], ALU.mult, ALU.mult,
                        [pb, rrb, bsm], [sb_])
                    d16, db_ = o16.next()
                    rope(s16, sb_, 0, 128, self.R64, cos64, sin64, d16, db_)
                    P.dma("pool", dst[c * 128:(c + 1) * 128, off:off + N], d16[:, 0:N], reads=[db_], writes=[bS[nm]])
                while later:
                    later.pop(0)()
                for tt in range(N // 128):
                    for (col, ncol, dst, nm) in ((VB, 256, self.VBs, "VB"), (VC, 256, self.VCs, "VC"),
                                                 (VD, 128, self.VDs, "VD")):
                        pt, pb = proj_tm(tt, col, ncol)
                        vt, vb = o16.next()
                        eg = evac_eng()
                        CP(P, eg, vt[:, 0:ncol], pt[:, 0:ncol], [pb], [vb])
                        P.dma("act" if eg == "act" else "pool", dst[koff + tt * 128: koff + (tt + 1) * 128, :], vt[:, 0:ncol], reads=[vb],
                              writes=[bS[nm]])

    def phase1b(self, l, need_ctx, x_own, ctx_in, w16, bw):
        P, W = self.P, self.W[l]
        bG = self.b_scr.setdefault("G", Buf("G"))
        with Scope(P) as S:
            A_l = S.sb("A_l", [128, D], F32); B_l = S.sb("B_l", [128, D], F32)
            bAB = Buf("AB")
            self.load_bc(A_l, self.modv[0, 1024:2048], bAB)
            self.load_bc(B_l, self.modv[0, 0:1024], bAB)
            xr = Ring(S.sb, "xt", 3, [128, D], F32)
            tr = Ring(S.sb, "t32", 2, [128, D], F32)
            hr = Ring(S.sb, "h16", 2, [128, D], BF16)
            hTr = Ring(S.sb, "hT", 2, [128, 8, 512], BF16)
            pTr = Ring(S.ps, "pT", 2, [128, 1024], BF16, excl=True)
            pp = Ring(S.ps, "pp", 4, [128, 512], F32, excl=True)
            gr = Ring(S.sb, "gt", 4, [128, 512], F32)
            chunks = [(x_own, ci * 512, 512, ci * 512, False) for ci in range(4)]
            if need_ctx:
                chunks.append((ctx_in, 0, 256, 2048, True))
            for (xsrc, row0, N, qoff, is_ctx) in chunks:
                hTt, hTb = hTr.next()
                if is_ctx:
                    self.load_bc(A_l, self.modv[1, 1024:2048], bAB)
                    self.load_bc(B_l, self.modv[1, 0:1024], bAB)
                self.make_hT(S, xsrc, row0, N, A_l, B_l, bAB, xr, tr, hr, pTr, hTt, hTb)
                for ct in range(32):
                    pt, pb = pp.next()
                    for k in range(8):
                        MM(P, pt[:, 0:N], w16[:, k, ct * 128:(ct + 1) * 128], hTt[:, k, 0:N], k == 0, k == 7, [bw, hTb],
                           [pb])
                    gt, gb = gr.next()
                    ACTV(P, gt[:, 0:N], pt[:, 0:N], AF.Sigmoid, [pb], [gb])
                    P.dma("act", self.G[ct, :, qoff:qoff + N], gt[:, 0:N], reads=[gb], writes=[bG])

    def phase2(self, l, need_ctx, yT, byT):
        P, W = self.P, self.W[l]
        bS, bc, bsm = self.b_scr, self.b_const, self.b_small
        with Scope(P) as S:
            ktr = Ring(S.sb, "kt", 4, [128, TK], BF16)
            qtr = Ring(S.sb, "qt", 2, [128, TQ], BF16)
            for t_, b_ in zip(qtr.t, qtr.b):
                MEMSET(P, "pool", t_[:], 0.0, [], [b_])
            vaug = [S.sb("vaug0", [128, 34, 128], BF16), S.sb("vaug1", [128, 34, 128], BF16)]
            bva = [Buf("vaug0"), Buf("vaug1")]
            MEMSET(P, "pool", vaug[0][:, :, 64:128], 1.0, [], [bva[0]])
            MEMSET(P, "pool", vaug[1][:, :, 0:64], 1.0, [], [bva[1]])
            pS = Ring(S.ps, "pS", 2, [128, 1024], F32, excl=True)
            pO = Ring(S.ps, "pO", 2, [128, 1024], F32, excl=True)
            ptr = Ring(S.sb, "pT16", 3, [128, 1024], BF16)
            bir = Ring(S.sb, "bias", 3, [128, 512], F32)
            t32r = Ring(S.sb, "s32", 2, [128, 512], F32)
            recr = Ring(S.sb, "rec", 3, [128, 1024], F32)
            zr = Ring(S.sb, "z", 4, [128, 1024], F32)
            z2r = Ring(S.sb, "z2", 2, [128, 1024], F32)
            rsr = Ring(S.sb, "rs", 2, [128, 512], F32)
            for t_, b_ in zip(z2r.t, z2r.b):
                MEMSET(P, "pool", t_[:], 0.0, [], [b_])

            def attend(qt, qb, kt, kb, hp, q0, ncol, tiles, scale):
                va, vab = vaug[hp // 64], bva[hp // 64]
                po, pob = pO.next()
                nb = (ncol + 511) // 512
                nt = len(tiles)

                def pv(p16, p16b, kti, i):
                    for j in range(nb):
                        w_ = min(512, ncol - j * 512)
                        MM(P, po[:, j * 512: j * 512 + w_], va[:, kti, :], p16[:, j * 512: j * 512 + w_],
                           i == 0, i == nt - 1, [vab, p16b], [pob])

                pend = None
                for i, (kti, bias) in enumerate(tiles):
                    ps, psb = pS.next()
                    for j in range(nb):
                        w_ = min(512, ncol - j * 512)
                        MM(P, ps[:, j * 512: j * 512 + w_], kt[:, kti * 128:(kti + 1) * 128],
                           qt[:, q0 + j * 512: q0 + j * 512 + w_], True, True, [kb, qb], [psb])
                    if pend is not None:
                        pv(*pend)
                    p16, p16b = ptr.next()
                    if bias is None:
                        ACTV(P, p16[:, 0:ncol], ps[:, 0:ncol], AF.Exp, [psb], [p16b], scale=scale)
                    else:
                        bt, bb_ = bir.next()
                        P.dma("sp", bt[:], bias, writes=[bb_])
                        s32, s32b = t32r.next()
                        STT(P, "dve", s32[:, 0:ncol], ps[:, 0:ncol], scale, bt[:, 0:ncol], ALU.mult, ALU.add,
                            [psb, bb_], [s32b])
                        ACTV(P, p16[:, 0:ncol], s32[:, 0:ncol], AF.Exp, [s32b], [p16b])
                    pend = (p16, p16b, kti, i)
                pv(*pend)
                return po, pob

            def normalise(po, pob, hp, ncol, dst, dstb):
                sp_ = 64 - hp
                rt, rb = recr.next()
                RECIP(P, rt[sp_:sp_ + 64, 0:ncol], po[sp_:sp_ + 64, 0:ncol], [pob], [rb])
                TT(P, "dve", dst, po[hp:hp + 64, 0:ncol], rt[sp_:sp_ + 64, 0:ncol], ALU.mult, [pob, rb], [dstb])

            def load_v(Vsrc, vcol, hp, vname):
                va, vab = vaug[hp // 64], bva[hp // 64]
                vo = 0 if hp == 0 else 64
                P.dma("sp", va[:, :, vo:vo + 64], Vsrc.rearrange("(kt p) c -> p kt c", p=128)[:, :, vcol:vcol + 64],
                      reads=[bS[vname]], writes=[vab])

            def load_k(Ksrc, base, rows, kname):
                kt, kb = ktr.next()
                MEMSET(P, "pool", kt[:], 0.0, [], [kb])
                P.dma("sp", kt[base:base + rows, :], Ksrc, reads=[bS[kname]], writes=[kb])
                return kt, kb

            def load_q(Qsrc, rows, qname):
                qt, qb = qtr.next()
                P.dma("sp", qt[0:rows, :], Qsrc, reads=[bS[qname]], writes=[qb])
                return qt, qb

            all_k = [(i, None) for i in range(34)]
            ctx_k = [(32, None), (33, None)]
            dense_groups = [(0, 1024, all_k), (1024, 1024, all_k)]
            if need_ctx:
                dense_groups.append((2048, 256, ctx_k))

            sc_a = 96 ** -0.5
            for h in range(4):
                hp = (h % 2) * 64
                kt, kb = load_k(self.KA[h], 0, 96, "KA")
                qt, qb = load_q(self.QA[h], 96, "QA")
                load_v(self.VA, h * 64, hp, "VA")
                for (q0, ncol, tiles) in dense_groups:
                    po, pob = attend(qt, qb, kt, kb, hp, q0, ncol, tiles, sc_a)
                    normalise(po, pob, hp, ncol, yT[hp:hp + 64, 0 + h // 2, q0:q0 + ncol], byT)
            sc_b = 64 ** -0.5
            for c in range(2):
                qt, qb = load_q(self.QBs[c * 128:(c + 1) * 128, :], 128, "QB")
                for e_ in range(2):
                    h = 2 * c + e_
                    hp = e_ * 64
                    kt, kb = load_k(self.KBs[h * 64:(h + 1) * 64, :], hp, 64, "KB")
                    load_v(self.VBs, h * 64, hp, "VB")
                    groups = []
                    for j in range(4):
                        st = (0, 1, 1, 2)[j]
                        tiles = []
                        for s in range(8):
                            lt = 4 * j - 2 + s
                            if lt < 0:
                                lt += 32
                            tiles.append((lt, W["nab"][st, s, h]))
                        tiles += ctx_k
                        groups.append((j * 512, 512, tiles))
                    if need_ctx:
                        groups.append((2048, 256, ctx_k))
                    for (q0, ncol, tiles) in groups:
                        po, pob = attend(qt, qb, kt, kb, hp, q0, ncol, tiles, sc_b)
                        normalise(po, pob, hp, ncol, yT[hp:hp + 64, 2 + c, q0:q0 + ncol], byT)
            sc_c = 32 ** -0.5

            def diff_epilogue(h, hp, q0, ncol, z0, z0b, z1, z1b):
                def run():
                    STT(P, "dve", z0[hp:hp + 64, 0:ncol], z1[hp:hp + 64, 0:ncol], self.lamt[hp:hp + 64, h:h + 1],
                        z0[hp:hp + 64, 0:ncol], ALU.mult, ALU.add, [z1b, z0b, bsm], [z0b])
                    z2, z2b = z2r.next()
                    TT(P, "pool", z2[hp:hp + 64, 0:ncol], z0[hp:hp + 64, 0:ncol], z0[hp:hp + 64, 0:ncol], ALU.mult,
                       [z0b], [z2b])
                    for j in range((ncol + 511) // 512):
                        w_ = min(512, ncol - j * 512)
                        pm_, pmb = pS.next()
                        MM(P, pm_[:, 0:w_], self.blk_f[:], z2[:, j * 512:j * 512 + w_], True, True, [z2b, bc], [pmb])
                        rs, rsb = rsr.next()
                        ACTV(P, rs[hp:hp + 64, 0:w_], pm_[hp:hp + 64, 0:w_], AF.Sqrt, [pmb, bc], [rsb],
                             bias=self.eps_col[hp:hp + 64, :], scale=1.0 / 64)
                        RECIP(P, rs[hp:hp + 64, 0:w_], rs[hp:hp + 64, 0:w_], [rsb], [rsb])
                        STT(P, "dve", yT[hp:hp + 64, 4 + h // 2, q0 + j * 512: q0 + j * 512 + w_],
                            z0[hp:hp + 64, j * 512:j * 512 + w_], self.gcols[hp:hp + 64, 0:1], rs[hp:hp + 64, 0:w_],
                            ALU.mult, ALU.mult, [z0b, rsb, bsm], [byT])
                return run

            pending = None
            for h in range(4):
                hp = (h % 2) * 64
                if h % 2 == 0:
                    qt, qb = load_q(self.QCs[(h // 2) * 128:(h // 2 + 1) * 128, :], 128, "QC")
                kms = [load_k(self.KCs[h * 64 + m * 32: h * 64 + (m + 1) * 32, :], hp + m * 32, 32, "KC")
                       for m in range(2)]
                load_v(self.VCs, h * 64, hp, "VC")
                for (q0, ncol, tiles) in dense_groups:
                    zt = []
                    for m in range(2):
                        po, pob = attend(qt, qb, kms[m][0], kms[m][1], hp, q0, ncol, tiles, sc_c)
                        z, zb = zr.next()
                        normalise(po, pob, hp, ncol, z[hp:hp + 64, 0:ncol], zb)
                        zt.append((z, zb))
                        if m == 0 and pending is not None:
                            pending()
                            pending = None
                    pending = diff_epilogue(h, hp, q0, ncol, zt[0][0], zt[0][1], zt[1][0], zt[1][1])
            if pending is not None:
                pending()
            sc_d = 64 ** -0.5
            kds = [load_k(self.KDs[e_ * 64:(e_ + 1) * 64, :], e_ * 64, 64, "KD") for e_ in range(2)]
            for c in range(2):
                qt, qb = load_q(self.QDs[c * 128:(c + 1) * 128, :], 128, "QD")
                for e_ in range(2):
                    hp = e_ * 64
                    load_v(self.VDs, e_ * 64, hp, "VD")
                    for (q0, ncol, tiles) in dense_groups:
                        po, pob = attend(qt, qb, kds[e_][0], kds[e_][1], hp, q0, ncol, tiles, sc_d)
                        normalise(po, pob, hp, ncol, yT[hp:hp + 64, 6 + c, q0:q0 + ncol], byT)

    def resid_ln(self, S, ps_halves, xsrc_ap, gbc, lng, lnb, bbc, out_ap, out_buf, rings):
        P = self.P
        xr, tr, zr, str_, mvr, orr = rings
        xt, xb = xr.next()
        P.dma("sp", xt[:], xsrc_ap[0], reads=xsrc_ap[1], writes=[xb])
        t32, tb = tr.next()
        for nh, (pt, pb) in enumerate(ps_halves):
            TT(P, "dve", t32[:, nh * 512:(nh + 1) * 512], pt[:, 0:512], gbc[:, nh * 512:(nh + 1) * 512], ALU.mult,
               [pb, bbc], [tb])
        z, zb = zr.next()
        STT(P, "dve", z[:], xt[:], ALPHA, t32[:], ALU.mult, ALU.add, [xb, tb], [zb])
        st, stb = str_.next()
        P.op("dve", lambda e: e.bn_stats(out=st[:, 0:6], in_=z[:, 0:512]), [zb], [stb])
        P.op("dve", lambda e: e.bn_stats(out=st[:, 6:12], in_=z[:, 512:1024]), [zb], [stb])
        mv, mvb = mvr.next()
        P.op("dve", lambda e: e.bn_aggr(out=mv[:, 0:2], in_=st[:]), [stb], [mvb])
        ACTV(P, mv[:, 2:3], mv[:, 1:2], AF.Sqrt, [mvb, self.b_const], [mvb], bias=self.eps_col[:], scale=1.0)
        RECIP(P, mv[:, 2:3], mv[:, 2:3], [mvb], [mvb])
        TS(P, "dve", z[:], z[:], mv[:, 0:1], mv[:, 2:3], ALU.subtract, ALU.mult, [zb, mvb], [zb])
        TT(P, "pool", z[:], z[:], lng[:], ALU.mult, [zb, bbc], [zb])
        ot, ob = orr.next()
        TT(P, "pool", ot[:], z[:], lnb[:], ALU.add, [zb, bbc], [ob])
        P.dma("pool", out_ap, ot[:], reads=[ob], writes=[out_buf])

    def ln_rings(self, S):
        return (Ring(S.sb, "lx", 2, [128, D], F32), Ring(S.sb, "lt", 2, [128, D], F32),
                Ring(S.sb, "lz", 2, [128, D], F32), Ring(S.sb, "lst", 2, [128, 12], F32),
                Ring(S.sb, "lmv", 2, [128, 4], F32), Ring(S.sb, "lo", 2, [128, D], F32))

    def phase3(self, l, need_ctx, x_own, ctx_in, yT, byT, wb16, wo16, bw):
        P, W = self.P, self.W[l]
        bS = self.b_scr
        bX = bS.setdefault("XL1", Buf("XL1"))
        with Scope(P) as S:
            g_l = S.sb("g_l", [128, D], F32); g_c = S.sb("g_c", [128, D], F32)
            lng = S.sb("lng", [128, D], F32); lnb = S.sb("lnb", [128, D], F32)
            bbc = Buf("bc3")
            self.load_bc(g_l, self.modv[0, 2048:3072], bbc)
            self.load_bc(g_c, self.modv[1, 2048:3072], bbc)
            P.dma("sp", lng[:], W["ln1_g"].partition_broadcast(128), writes=[bbc])
            P.dma("sp", lnb[:], W["ln1_b"].partition_broadcast(128), writes=[bbc])
            pp = Ring(S.ps, "pp", 3, [128, 512], F32, excl=True)
            po = Ring(S.ps, "po", 4, [128, 512], F32, excl=True)
            gr = Ring(S.sb, "gt", 4, [128, 512], F32)
            accr = Ring(S.sb, "acc", 2, [128, 512], F32)
            tmr = Ring(S.sb, "tm", 2, [128, 512], F32)
            aTr = Ring(S.sb, "accT", 2, [128, 8, 512], BF16)
            rings = self.ln_rings(S)
            chunks = [(x_own, ci * 512, 512, ci * 512, g_l) for ci in range(4)]
            if need_ctx:
                chunks.append((ctx_in, 0, 256, 2048, g_c))
            for (xsrc, row0, N, qoff, gbc) in chunks:
                aT, aTb = aTr.next()
                for c in range(8):
                    acc, accb = accr.next()
                    for i in range(4):
                        pt, pb = pp.next()
                        for k in range(2):
                            MM(P, pt[:, 0:N], wb16[:, i * 2 + k, c * 128:(c + 1) * 128], yT[:, i * 2 + k, qoff:qoff + N],
                               k == 0, k == 1, [bw, byT], [pb])
                        gt, gb = gr.next()
                        P.dma("sp", gt[:, 0:N], self.G[i * 8 + c, :, qoff:qoff + N], reads=[bS["G"]], writes=[gb])
                        if i == 0:
                            TT(P, "dve", acc[:, 0:N], pt[:, 0:N], gt[:, 0:N], ALU.mult, [pb, gb], [accb])
                        else:
                            tm, tmb = tmr.next()
                            TT(P, "dve", tm[:, 0:N], pt[:, 0:N], gt[:, 0:N], ALU.mult, [pb, gb], [tmb])
                            if i < 3:
                                TT(P, "dve" if i == 1 else "pool", acc[:, 0:N], acc[:, 0:N], tm[:, 0:N], ALU.add,
                                   [accb, tmb], [accb])
                            else:
                                TT(P, "pool", aT[:, c, 0:N], acc[:, 0:N], tm[:, 0:N], ALU.add, [accb, tmb], [aTb])
                for tt in range(N // 128):
                    halves = []
                    for nh in range(2):
                        pt, pb = po.next()
                        for c in range(8):
                            MM(P, pt[:, 0:512], aT[:, c, tt * 128:(tt + 1) * 128], wo16[:, c, nh * 512:(nh + 1) * 512],
                               c == 0, c == 7, [aTb, bw], [pb])
                        halves.append((pt, pb))
                    r0 = row0 + tt * 128
                    self.resid_ln(S, halves, (xsrc["ap"][r0:r0 + 128, :], xsrc["deps"]), gbc, lng, lnb, bbc,
                                  self.XL1[qoff + tt * 128: qoff + (tt + 1) * 128, :], bX, rings)

    def phase4a(self, l, need_ctx, pre):
        P, W = self.P, self.W[l]
        bS = self.b_scr
        bA = bS.setdefault("ACTT", Buf("ACTT"))
        with Scope(P) as S:
            w16 = S.sb("wgu16", [128, 8, 2 * DFF], BF16)
            wv = W["w_gu"].rearrange("(k p) n -> p k n", p=128)
            JB = 4
            bwj = {}
            wblocks = []
            for j0 in range(0, 22, JB):
                j1 = min(22, j0 + JB)
                bb_ = Buf("wgu%d" % j0)
                wblocks.append((j0, j1, bb_))
                for j in range(j0, j1):
                    bwj[j] = bb_

            def issue_block(bi):
                j0, j1, bb_ = wblocks[bi]
                P.dma("pool", w16[:, :, j0 * 128:j1 * 128], wv[:, :, j0 * 128:j1 * 128], writes=[bb_])
                P.dma("pool", w16[:, :, DFF + j0 * 128:DFF + j1 * 128], wv[:, :, DFF + j0 * 128:DFF + j1 * 128],
                      writes=[bb_])

            issue_block(0)
            issue_block(1)
            issue_block(2)
            next_block = [3]
            wd16, bwd, wdsrc = pre
            wdv = wdsrc.rearrange("(j p) n -> p j n", p=128)
            wd_issued = [False]

            def issue_wd():
                if not wd_issued[0]:
                    wd_issued[0] = True
                    for j0 in range(0, 22, 6):
                        j1 = min(22, j0 + 6)
                        P.dma("pool", wd16[:, j0:j1, :], wdv[:, j0:j1, :], writes=[bwd])
            A_l = S.sb("A_l", [128, D], F32); B_l = S.sb("B_l", [128, D], F32)
            bAB = Buf("AB")
            self.load_bc(A_l, self.modv[0, 4096:5120], bAB)
            self.load_bc(B_l, self.modv[0, 3072:4096], bAB)
            xr = Ring(S.sb, "xt", 2, [128, D], F32)
            tr = Ring(S.sb, "t32", 2, [128, D], F32)
            hr = Ring(S.sb, "h16", 2, [128, D], BF16)
            hTr = Ring(S.sb, "hT", 2, [128, 8, 512], BF16)
            pTr = Ring(S.ps, "pT", 2, [128, 1024], BF16, excl=True)
            pg = Ring(S.ps, "pg", 3, [128, 512], F32, excl=True)
            pu = Ring(S.ps, "pu", 3, [128, 512], F32, excl=True)
            sr = Ring(S.sb, "sil", 3, [128, 512], F32)
            ar = Ring(S.sb, "a16", 4, [128, 512], BF16)
            chunks = [(ci * 512, 512, False) for ci in range(4)]
            if need_ctx:
                chunks.append((2048, 256, True))
            for (qoff, N, is_ctx) in chunks:
                hTt, hTb = hTr.next()
                if is_ctx:
                    self.load_bc(A_l, self.modv[1, 4096:5120], bAB)
                    self.load_bc(B_l, self.modv[1, 3072:4096], bAB)
                self.make_hT(S, dict(ap=self.XL1, deps=[bS["XL1"]]), qoff, N, A_l, B_l, bAB, xr, tr, hr, pTr, hTt, hTb)
                for j in range(22):
                    if next_block[0] < len(wblocks) and j % 2 == 1:
                        issue_block(next_block[0])
                        next_block[0] += 1
                    bw = bwj[j]
                    ptg, pbg = pg.next()
                    for k in range(8):
                        MM(P, ptg[:, 0:N], w16[:, k, j * 128:(j + 1) * 128], hTt[:, k, 0:N], k == 0, k == 7, [bw, hTb],
                           [pbg])
                    ptu, pbu = pu.next()
                    for k in range(8):
                        MM(P, ptu[:, 0:N], w16[:, k, DFF + j * 128: DFF + (j + 1) * 128], hTt[:, k, 0:N], k == 0, k == 7,
                           [bw, hTb], [pbu])
                    st, sb_ = sr.next()
                    ACTV(P, st[:, 0:N], ptg[:, 0:N], AF.Silu, [pbg], [sb_])
                    at, ab = ar.next()
                    TT(P, "dve", at[:, 0:N], ptu[:, 0:N], st[:, 0:N], ALU.mult, [pbu, sb_], [ab])
                    P.dma("act", self.ACTT[j, :, qoff:qoff + N], at[:, 0:N], reads=[ab], writes=[bA])
                issue_wd()

    def phase4b(self, l, need_ctx, y_out, ctx_out, wd16, bw):
        P, W = self.P, self.W[l]
        bS = self.b_scr
        with Scope(P) as S:
            g_l = S.sb("g_l", [128, D], F32); g_c = S.sb("g_c", [128, D], F32)
            lng = S.sb("lng", [128, D], F32); lnb = S.sb("lnb", [128, D], F32)
            bbc = Buf("bc4")
            self.load_bc(g_l, self.modv[0, 5120:6144], bbc)
            self.load_bc(g_c, self.modv[1, 5120:6144], bbc)
            P.dma("sp", lng[:], W["ln2_g"].partition_broadcast(128), writes=[bbc])
            P.dma("sp", lnb[:], W["ln2_b"].partition_broadcast(128), writes=[bbc])
            atr = Ring(S.sb, "aT", 2, [128, 22, 512], BF16)
            po = Ring(S.ps, "po", 6, [128, 512], F32, excl=True)
            rings = self.ln_rings(S)
            chunks = [(ci * 512, 512, g_l, y_out[0], ci * 512, y_out[1]) for ci in range(4)]
            if need_ctx:
                chunks.append((2048, 256, g_c, ctx_out[0], 0, ctx_out[1]))
            for (qoff, N, gbc, dst, drow, dbuf) in chunks:
                at, ab = atr.next()
                P.dma("sp", at[:, :, 0:N], self.ACTT[:, :, qoff:qoff + N].rearrange("j p t -> p j t"),
                      reads=[bS["ACTT"]], writes=[ab])
                for tt in range(N // 128):
                    halves = []
                    for nh in range(2):
                        pt, pb = po.next()
                        for j in range(22):
                            MM(P, pt[:, 0:512], at[:, j, tt * 128:(tt + 1) * 128], wd16[:, j, nh * 512:(nh + 1) * 512],
                               j == 0, j == 21, [ab, bw], [pb])
                        halves.append((pt, pb))
                    r0 = qoff + tt * 128
                    self.resid_ln(S, halves, (self.XL1[r0:r0 + 128, :], [bS["XL1"]]), gbc, lng, lnb, bbc,
                                  dst[drow + tt * 128: drow + (tt + 1) * 128, :], dbuf, rings)


def _rot_matrix(Dr):
    n_f = Dr // 4
    R = np.zeros((Dr, Dr), np.float32)
    for a in range(2):
        for f in range(n_f):
            i1 = a * 2 * n_f + f
            i2 = i1 + n_f
            R[i2, i1] = -1.0
            R[i1, i2] = 1.0
    return R


def _consts():
    c = np.zeros((5, 128, 128), np.float32)
    c[0] = np.eye(128, dtype=np.float32)
    r32, r64 = _rot_matrix(32), _rot_matrix(64)
    for i in range(4):
        c[1, i * 32:(i + 1) * 32, i * 32:(i + 1) * 32] = r32
    for i in range(2):
        c[2, i * 64:(i + 1) * 64, i * 64:(i + 1) * 64] = r64
        c[4, i * 64:(i + 1) * 64, i * 64:(i + 1) * 64] = 1.0
    c[3] = 1.0
    return c


def _rope_tables(half):
    out = np.zeros((4, 128, TK), np.float32)
    out[0, :, 4096:] = 1.0
    out[2, :, 4096:] = 1.0
    tg = np.concatenate([np.arange(2048) + half * 2048, np.arange(2048) + (1 - half) * 2048])
    pos = np.stack([tg // 64, tg % 64], axis=0).astype(np.float32)
    for ti, Dr in ((0, 32), (2, 64)):
        n_f = Dr // 4
        inv = (np.float32(10000.0) ** (-np.arange(n_f, dtype=np.float32) / np.float32(n_f))).astype(np.float32)
        p = np.arange(128)
        i = p % Dr
        a = i // (2 * n_f)
        f = i % n_f
        ang = (pos[a, :] * inv[f][:, None]).astype(np.float32)
        out[ti, :, :4096] = np.cos(ang)
        out[ti + 1, :, :4096] = np.sin(ang)
    return out


def _na_bias(rpb_l, half):
    out = np.empty((3, 8, 4, 128, 512), np.float32)
    kk = np.arange(128)
    qq = np.arange(512)
    for st, j in ((0, 0), (1, 1), (2, 3)):
        qr = 8 * j + qq // 64 + 32 * half
        qc = qq % 64
        rs = np.clip(qr - 4, 0, 56)
        cs = np.clip(qc - 8, 0, 48)
        for s in range(8):
            lt = 4 * j - 2 + s
            if lt < 0:
                lt += 32
            if lt < 16:
                kr = 2 * lt + kk // 64 + 32 * half
            else:
                kr = 2 * (lt - 16) + kk // 64 + 32 * (1 - half)
            kc = kk % 64
            inw = ((kr[:, None] >= rs[None]) & (kr[:, None] < rs[None] + 8) &
                   (kc[:, None] >= cs[None]) & (kc[:, None] < cs[None] + 16))
            ro = np.clip(kr[:, None] - qr[None] + 7, 0, 14)
            co = np.clip(kc[:, None] - qc[None] + 15, 0, 30)
            for h in range(4):
                out[st, s, h] = np.where(inw, rpb_l[h][ro, co], np.float32(NEG))
    return out


def _layer_inputs(inp, l, nab_by_half):
    f = lambda a: np.ascontiguousarray(a, dtype=np.float32)
    w_in = inp["w_in"][l]
    qd = w_in[:, 2208:2464].reshape(D, 4, 64)[:, [0, 2, 1, 3], :].reshape(D, 256)
    w_qkv = np.concatenate([w_in[:, :2208], qd, w_in[:, 2464:2720]], axis=1)
    wkv = inp["w_kv_up"][l].reshape(256, 4, 2, 64)
    wkv = np.concatenate([wkv[:, :, 0, :].reshape(256, 256), wkv[:, :, 1, :].reshape(256, 256)], axis=1)
    wb = np.array(inp["w_branch"][l], dtype=np.float32)
    wb[3] = wb[3].reshape(4, 64, D)[[0, 2, 1, 3]].reshape(256, D)
    s = "_%d" % l
    common = {
        "w_ada" + s: f(inp["w_ada"][l]), "b_ada" + s: f(inp["b_ada"][l]),
        "w_qkv" + s: f(w_qkv), "w_gate" + s: f(w_in[:, 2720:]),
        "gqaT" + s: f(inp["g_q_a"][l].reshape(3, 128).T), "w_q_up" + s: f(inp["w_q_up"][l]),
        "gkvT" + s: f(inp["g_kv_a"][l].reshape(2, 128).T), "w_kv_up" + s: f(wkv),
        "lamv" + s: f(np.stack([inp["lam_q1"][l].ravel(), inp["lam_k1"][l].ravel(),
                                inp["lam_q2"][l].ravel(), inp["lam_k2"][l].ravel()])),
        "gcols" + s: f(np.stack([np.tile(inp["g_sub"][l], 2), np.tile(inp["g_qn"][l], 2),
                                 np.tile(inp["g_kn"][l], 2)], axis=1)),
        "w_branch" + s: f(wb), "w_out" + s: f(inp["w_out"][l]),
        "ln1_g" + s: f(inp["ln1_g"][l]), "ln1_b" + s: f(inp["ln1_b"][l]),
        "w_gu" + s: f(inp["w_gate_up"][l]), "w_down" + s: f(inp["w_down"][l]),
        "ln2_g" + s: f(inp["ln2_g"][l]), "ln2_b" + s: f(inp["ln2_b"][l]),
    }
    per_half = []
    for half in range(2):
        d = dict(common)
        d["nab" + s] = nab_by_half[half]
        per_half.append(d)
    return per_half


_NC_CACHE = {}


def _get_nc(layers):
    key = tuple(layers)
    if key not in _NC_CACHE:
        _NC_CACHE[key] = Builder(layers).build()
    return _NC_CACHE[key]


def kernel(**inputs):
    inp = {k: np.asarray(v) for k, v in inputs.items()}
    x, c, ctx, c_ctx = inp["x"], inp["c"], inp["ctx"], inp["c_ctx"]
    consts = _consts()
    ropes = [_rope_tables(0), _rope_tables(1)]
    lays = []
    for l in range(DEPTH):
        nab = [_na_bias(inp["rpb"][l], 0), _na_bias(inp["rpb"][l], 1)]
        lays.append(_layer_inputs(inp, l, nab))
    sel = [np.zeros((128, D), np.uint32), np.ones((128, D), np.uint32)]
    in_maps = []
    for core in range(8):
        b, half = core // 2, core % 2
        m = {}
        for l in range(DEPTH):
            m.update(lays[l][half])
        m["x_own"] = np.ascontiguousarray(x[b, half * 2048:(half + 1) * 2048], dtype=np.float32)
        m["x_oth"] = np.ascontiguousarray(x[b, (1 - half) * 2048:(2 - half) * 2048], dtype=np.float32)
        m["ctx_in"] = np.ascontiguousarray(ctx[b], dtype=np.float32)
        cT = np.stack([c[b].reshape(8, 128).T, c_ctx.reshape(8, 128).T], axis=2).reshape(128, 16)
        m["cT"] = np.ascontiguousarray(cT, dtype=np.float32)
        m["consts"] = consts
        m["rope"] = ropes[half]
        m["selmask"] = sel[half]
        in_maps.append(m)
    nc = _get_nc(list(range(DEPTH)))
    res = run_bass_kernel_spmd(nc, in_maps, core_ids=list(range(8)))
    out = np.empty((4, 4096, D), np.float32)
    for core in range(8):
        out[core // 2, (core % 2) * 2048:(core % 2 + 1) * 2048] = np.asarray(res.results[core]["y_out"])
    return out
```
